# Optimizing a Trainium2 kernel written in Bass

```python
import math
import jax
import jax.numpy as jnp
from jax import lax
import numpy as np

D_MODEL = 1024
BATCH = 8
SEQ = 4096
DEPTH = 1

MOBA_HEADS = 8
MOBA_HEAD_DIM = 64
MOBA_BLOCK = 256
MOBA_TOPK = 3
MOBA_Q_CHUNK = 32
GLA_HEADS = 4
GLA_DK = D_MODEL // 2 // GLA_HEADS
GLA_DV = D_MODEL // GLA_HEADS
GLA_GATE_RANK = 16
GLA_GATE_TAU = 16.0
GLA_CHUNK = 64
T5_BUCKETS = 32
T5_MAX_DIST = 128
D_FF = 4 * D_MODEL
EPS = 1e-6

MOBA_WIDTH = MOBA_HEADS * MOBA_HEAD_DIM
GLA_K_WIDTH = GLA_HEADS * GLA_DK
GLA_V_WIDTH = GLA_HEADS * GLA_DV
IN_SPLIT_SIZES = (MOBA_WIDTH, MOBA_WIDTH, MOBA_WIDTH,
                  GLA_K_WIDTH, GLA_K_WIDTH, GLA_V_WIDTH, GLA_GATE_RANK, GLA_V_WIDTH,
                  D_MODEL, D_MODEL)
IN_SPLIT_POINTS = tuple(sum(IN_SPLIT_SIZES[:i + 1]) for i in range(len(IN_SPLIT_SIZES) - 1))
D_IN = sum(IN_SPLIT_SIZES)

kernel_name = "hybrid_moba_gla_gated_block"


def rms_norm(x, gain):
    xf = x.astype(jnp.float32)
    y = xf * lax.rsqrt(jnp.mean(xf * xf, axis=-1, keepdims=True) + EPS) * gain.astype(jnp.float32)
    return y.astype(x.dtype)


def t5_bucket(dist):
    max_exact = T5_BUCKETS // 2
    d = jnp.maximum(dist, 0)
    large = max_exact + (jnp.log(jnp.maximum(d, 1).astype(jnp.float32) / max_exact)
                         / math.log(T5_MAX_DIST / max_exact)
                         * (T5_BUCKETS - max_exact)).astype(jnp.int32)
    large = jnp.minimum(large, T5_BUCKETS - 1)
    return jnp.where(d < max_exact, d, large)


def moba_attention(q, k, v, t5_bias):
    B, S, H, Dh = q.shape
    q = q.transpose(0, 2, 1, 3)
    k = k.transpose(0, 2, 1, 3)
    v = v.transpose(0, 2, 1, 3)
    nb = -(-S // MOBA_BLOCK)
    s_pad = nb * MOBA_BLOCK
    pad = ((0, 0), (0, 0), (0, s_pad - S), (0, 0))
    kb = jnp.pad(k, pad).reshape(B, H, nb, MOBA_BLOCK, Dh)
    vb = jnp.pad(v, pad).reshape(B, H, nb, MOBA_BLOCK, Dh)
    scale = Dh ** -0.5
    q_block = jnp.arange(S) // MOBA_BLOCK
    n_sel = min(MOBA_TOPK, nb - 1)
    if n_sel > 0:
        k_mean = kb.astype(jnp.float32).mean(axis=3)
        gate = jnp.einsum("bhsd,bhnd->bhsn", q.astype(jnp.float32), k_mean)
        fully_past = jnp.arange(nb)[None, :] < q_block[:, None]
        gate = jnp.where(fully_past, gate, -jnp.inf)
        _, sel_idx = lax.top_k(gate, n_sel)
    else:
        sel_idx = jnp.zeros((B, H, S, 0), jnp.int32)
    n_chunks = S // MOBA_Q_CHUNK
    q_c = q.reshape(B, H, n_chunks, MOBA_Q_CHUNK, Dh).transpose(2, 0, 1, 3, 4)
    idx_c = sel_idx.reshape(B, H, n_chunks, MOBA_Q_CHUNK, n_sel).transpose(2, 0, 1, 3, 4)
    bias_hb = t5_bias.T
    b_ix = jnp.arange(B)[:, None, None, None]
    h_ix = jnp.arange(H)[None, :, None, None]
    h_ix5 = jnp.arange(H)[None, :, None, None, None]
    blk = jnp.arange(MOBA_BLOCK)

    def chunk_fn(args):
        ci, qc, ic = args
        c0 = ci * MOBA_Q_CHUNK
        t = c0 + jnp.arange(MOBA_Q_CHUNK)
        ob = c0 // MOBA_BLOCK
        k_own = lax.dynamic_index_in_dim(kb, ob, axis=2, keepdims=False)
        v_own = lax.dynamic_index_in_dim(vb, ob, axis=2, keepdims=False)
        d_own = t[:, None] - (ob * MOBA_BLOCK + blk)[None, :]
        l_own = jnp.einsum("bhqd,bhkd->bhqk", qc, k_own).astype(jnp.float32) * scale
        l_own = l_own + bias_hb[:, t5_bucket(d_own)].astype(jnp.float32)
        l_own = jnp.where(d_own >= 0, l_own, -jnp.inf)
        k_sel = kb[b_ix, h_ix, ic]
        v_sel = vb[b_ix, h_ix, ic]
        l_sel = jnp.einsum("bhqd,bhqjkd->bhqjk", qc, k_sel).astype(jnp.float32) * scale
        d_sel = t[None, None, :, None, None] - (ic[..., None] * MOBA_BLOCK + blk)
        l_sel = l_sel + bias_hb[h_ix5, t5_bucket(d_sel)].astype(jnp.float32)
        l_sel = jnp.where((ic < ob)[..., None], l_sel, -jnp.inf)
        l_sel = l_sel.reshape(B, H, MOBA_Q_CHUNK, n_sel * MOBA_BLOCK)
        p = jax.nn.softmax(jnp.concatenate([l_sel, l_own], axis=-1), axis=-1)
        p_sel = p[..., :n_sel * MOBA_BLOCK].astype(v.dtype)
        p_own = p[..., n_sel * MOBA_BLOCK:].astype(v.dtype)
        o = (jnp.einsum("bhqj,bhqjd->bhqd", p_sel,
                        v_sel.reshape(B, H, MOBA_Q_CHUNK, n_sel * MOBA_BLOCK, Dh))
             + jnp.einsum("bhqk,bhkd->bhqd", p_own, v_own))
        return o.astype(q.dtype)

    out = lax.map(chunk_fn, (jnp.arange(n_chunks, dtype=jnp.int32), q_c, idx_c))
    return out.transpose(1, 0, 3, 2, 4).reshape(B, S, H * Dh)


def gla_attention(q, k, v, log_a):
    B, S, H, dk = q.shape
    C = GLA_CHUNK
    nc = S // C

    def chunks(t):
        return t.astype(jnp.float32).reshape(B, nc, C, H, -1).transpose(1, 0, 3, 2, 4)

    qc = chunks(q) * dk ** -0.5
    kc = chunks(k)
    vc = chunks(v)
    b = jnp.cumsum(chunks(log_a), axis=3)
    b_last = b[..., -1:, :]
    q_dec = qc * jnp.exp(b)
    k_inv = kc * jnp.exp(-b)
    k_dec = kc * jnp.exp(b_last - b)
    causal = jnp.tril(jnp.ones((C, C), dtype=bool))
    attn = jnp.where(causal, jnp.einsum("nbhcd,nbhed->nbhce", q_dec, k_inv), 0.0)
    o_intra = jnp.einsum("nbhce,nbhev->nbhcv", attn, vc)

    def step(state, xs):
        qd, kd, vv, bl = xs
        o_inter = jnp.einsum("bhcd,bhdv->bhcv", qd, state)
        state = jnp.exp(bl)[:, :, 0, :, None] * state + jnp.einsum("bhcd,bhcv->bhdv", kd, vv)
        return state, o_inter

    state0 = jnp.zeros((B, H, dk, v.shape[-1]), jnp.float32)
    _, o_inter = lax.scan(step, state0, (q_dec, k_dec, vc, b_last))
    o = o_intra + o_inter
    return o.transpose(1, 0, 3, 2, 4).reshape(B, S, H, v.shape[-1])


def setup_inputs(seed: int = 0) -> dict:
    key = jax.random.key(seed)
    ks = jax.random.split(key, 16)

    def nrm(k, shape, scale):
        return jax.random.normal(k, shape, jnp.float32) * scale

    return {
        "x": nrm(ks[0], (BATCH, SEQ, D_MODEL), 1.0),
        "norm1_g": 1.0 + nrm(ks[1], (DEPTH, D_MODEL), 0.05),
        "w_in": nrm(ks[2], (DEPTH, D_MODEL, D_IN), D_MODEL ** -0.5),
        "moba_q_norm_g": 1.0 + nrm(ks[3], (DEPTH, MOBA_HEAD_DIM), 0.05),
        "moba_k_norm_g": 1.0 + nrm(ks[4], (DEPTH, MOBA_HEAD_DIM), 0.05),
        "t5_bias": nrm(ks[5], (T5_BUCKETS, MOBA_HEADS), 0.5),
        "gla_gate_up": nrm(ks[6], (DEPTH, GLA_GATE_RANK, GLA_K_WIDTH), GLA_GATE_RANK ** -0.5),
        "gla_gate_bias": nrm(ks[7], (DEPTH, GLA_K_WIDTH), 0.1),
        "gla_out_norm_g": 1.0 + nrm(ks[8], (DEPTH, GLA_DV), 0.05),
        "w_branch_moba": nrm(ks[9], (DEPTH, MOBA_WIDTH, D_MODEL), MOBA_WIDTH ** -0.5),
        "w_branch_gla": nrm(ks[10], (DEPTH, GLA_V_WIDTH, D_MODEL), GLA_V_WIDTH ** -0.5),
        "w_out": nrm(ks[11], (DEPTH, D_MODEL, D_MODEL), D_MODEL ** -0.5),
        "norm2_g": 1.0 + nrm(ks[12], (DEPTH, D_MODEL), 0.05),
        "w_up": nrm(ks[13], (DEPTH, D_MODEL, D_FF), D_MODEL ** -0.5),
        "w_down": nrm(ks[14], (DEPTH, D_FF, D_MODEL), D_FF ** -0.5),
    }


def reference(x, norm1_g, w_in, moba_q_norm_g, moba_k_norm_g, t5_bias, gla_gate_up,
              gla_gate_bias, gla_out_norm_g, w_branch_moba, w_branch_gla, w_out,
              norm2_g, w_up, w_down):
    B, S, _ = x.shape
    h = x
    for layer in range(DEPTH):
        n = rms_norm(h, norm1_g[layer])
        proj = n @ w_in[layer]
        (mq, mk, mv, gq, gk, gv, g_low, g_out, gate_moba, gate_gla) = jnp.split(
            proj, IN_SPLIT_POINTS, axis=-1)
        mq = rms_norm(mq.reshape(B, S, MOBA_HEADS, MOBA_HEAD_DIM), moba_q_norm_g[layer])
        mk = rms_norm(mk.reshape(B, S, MOBA_HEADS, MOBA_HEAD_DIM), moba_k_norm_g[layer])
        mv = mv.reshape(B, S, MOBA_HEADS, MOBA_HEAD_DIM)
        y_moba = moba_attention(mq, mk, mv, t5_bias)
        log_a = jax.nn.log_sigmoid(
            (g_low @ gla_gate_up[layer] + gla_gate_bias[layer]).astype(jnp.float32)) / GLA_GATE_TAU
        y_gla = gla_attention(gq.reshape(B, S, GLA_HEADS, GLA_DK),
                              gk.reshape(B, S, GLA_HEADS, GLA_DK),
                              gv.reshape(B, S, GLA_HEADS, GLA_DV),
                              log_a.reshape(B, S, GLA_HEADS, GLA_DK))
        y_gla = rms_norm(y_gla, gla_out_norm_g[layer]).astype(x.dtype)
        y_gla = (y_gla * jax.nn.silu(g_out.reshape(B, S, GLA_HEADS, GLA_DV))).reshape(B, S, GLA_V_WIDTH)
        mixed = (jax.nn.sigmoid(gate_moba) * (y_moba @ w_branch_moba[layer])
                 + jax.nn.sigmoid(gate_gla) * (y_gla @ w_branch_gla[layer]))
        h = h + mixed @ w_out[layer]
        n2 = rms_norm(h, norm2_g[layer])
        h = h + jnp.square(jax.nn.relu(n2 @ w_up[layer])) @ w_down[layer]
    return h
```

```python
import numpy as np
import ml_dtypes
import concourse.bass as bass
import concourse.mybir as mybir
from concourse.bass_utils import run_bass_kernel_spmd

F32 = mybir.dt.float32
BF16 = mybir.dt.bfloat16
AF = mybir.ActivationFunctionType
ALU = mybir.AluOpType
AX = mybir.AxisListType


class Prog:
    def __init__(self, nc):
        self.nc = nc
        self.ins = []
        self.dma_cum = {}

    def op(self, eng, fn, reads=(), writes=()):
        writes = list(writes) + [r for r in reads if r.startswith("ps") and r not in writes]
        self.ins.append(dict(eng=eng, fn=fn, reads=list(reads), writes=list(writes), dma=None))

    def dma(self, eng, out, in_, sem, reads=(), writes=(), **kw):
        v = self.dma_cum.get(sem, 0) + 16
        self.dma_cum[sem] = v
        self.ins.append(dict(eng=eng, fn=lambda e: e.dma_start(out=out, in_=in_, **kw),
                             reads=list(reads), writes=list(writes), dma=(sem, v)))

    def finish(self, eng, resources):
        self.ins.append(dict(eng=eng, fn=None, reads=list(resources), writes=[], dma=None))

    def emit(self):
        nc = self.nc
        ins = self.ins
        n = len(ins)
        last_w = {}
        readers = {}
        deps = [None] * n
        needed = [False] * n
        for i, I in enumerate(ins):
            d = set()
            for r in I["reads"]:
                j = last_w.get(r)
                if j is not None:
                    d.add((j, "raw"))
            for r in I["writes"]:
                j = last_w.get(r)
                if j is not None:
                    d.add((j, "waw"))
                for j in readers.get(r, ()):
                    d.add((j, "war"))
            for r in I["reads"]:
                readers.setdefault(r, []).append(i)
            for r in I["writes"]:
                last_w[r] = i
                readers[r] = []
            dd = {}
            for j, kind in d:
                if j == i:
                    continue
                J = ins[j]
                if J["dma"] is None and I["dma"] is None and J["eng"] == I["eng"]:
                    if I["eng"] == "pe" or kind != "raw":
                        continue
                dd[j] = True
            deps[i] = list(dd)
            for j in dd:
                needed[j] = True
        engs = ["pe", "act", "dve", "pool", "sp"]
        sems = {e: nc.alloc_semaphore("sem_" + e) for e in engs}
        dsems = {k: nc.alloc_semaphore("dsem_" + k) for k in self.dma_cum}
        cnt = {e: 0 for e in engs}
        token = [None] * n
        for i, I in enumerate(ins):
            if I["dma"] is not None:
                token[i] = ("d:" + I["dma"][0], I["dma"][1])
            elif I["fn"] is not None and needed[i]:
                cnt[I["eng"]] += 1
                token[i] = ("e:" + I["eng"], cnt[I["eng"]])
        streams = {e: [] for e in engs}
        waited = {e: {} for e in engs}
        dma_issued = {}
        n_wait = 0
        for i, I in enumerate(ins):
            e = I["eng"]
            want = {}
            for j in deps[i]:
                tk = token[j]
                if tk is None:
                    continue
                key, val = tk
                if key.startswith("d:"):
                    val = max(val, dma_issued.get(key, 0))
                if want.get(key, 0) < val:
                    want[key] = val
            for key, val in want.items():
                if waited[e].get(key, 0) >= val:
                    continue
                waited[e][key] = val
                s = dsems[key[2:]] if key.startswith("d:") else sems[key[2:]]
                streams[e].append(("wait", s, val))
                n_wait += 1
            if I["dma"] is not None:
                dma_issued["d:" + I["dma"][0]] = I["dma"][1]
                streams[e].append(("dma", I["fn"], dsems[I["dma"][0]]))
            elif I["fn"] is not None:
                streams[e].append(("op", I["fn"], sems[e] if needed[i] else None))
        self.stats = dict(n_ins=n, n_wait=n_wait, per_eng={e: len(streams[e]) for e in engs})

        def run(e_obj, lst):
            for it in lst:
                if it[0] == "wait":
                    e_obj.wait_ge(it[1], it[2])
                elif it[0] == "dma":
                    it[1](e_obj).then_inc(it[2], 16)
                else:
                    r = it[1](e_obj)
                    if it[2] is not None:
                        r.then_inc(it[2], 1)

        with nc.Block() as block:
            @block.tensor
            def _(e):
                run(e, streams["pe"])

            @block.scalar
            def _(e):
                run(e, streams["act"])

            @block.vector
            def _(e):
                run(e, streams["dve"])

            @block.gpsimd
            def _(e):
                run(e, streams["pool"])

            @block.sync
            def _(e):
                run(e, streams["sp"])


S = 4096
D = 1024
T = 256
NTILE = S // T
DIN = 6672
DFF = 4096
NB = 3
EPS = 1e-6
NCHUNK = 35
WIN_COLS = [0, 512, 1024, 1536, 2048, 2560, 3072, 3600, 4112, 4624, 5136, 5648, 6160]
GLOW_COL = 3584

CF = {}
_o = 0
for _n, _w in [("g1T", 8), ("g2T", 8), ("gonT", 8), ("gqc", 1), ("gkc", 1), ("t5far", 8), ("vbvec", 31),
               ("tri_incl", 128), ("tri_strict", 128), ("causal", 128), ("gup", 512), ("ones", 64),
               ("identf", 128)]:
    CF[_n] = (_o, _w)
    _o += _w
NCF = _o
CB = {}
_o = 0
for _n, _w in [("identb", 128), ("blockones", 128), ("blockind", 2048)]:
    CB[_n] = (_o, _w)
    _o += _w
NCB = _o


class _Stop(Exception):
    pass


def build_nc(ntiles=NTILE, dbg=None, dbg_tile=0, stage=99):
    dbg = dbg or []
    nc = bass.Bass("TRN2", target_bir_lowering=False)

    def din(name, shape, dt=F32):
        return nc.dram_tensor(name, list(shape), dt, kind="ExternalInput").ap()

    x = din("x", [S, D])
    w_in = din("w_in", [D, DIN])
    w_bm = din("w_bm", [512, D])
    w_bg = din("w_bg", [D, D])
    w_out = din("w_out", [D, D])
    w_up = din("w_up", [D, DFF])
    w_down = din("w_down", [DFF, D])
    cst_f = din("cst_f", [128, NCF])
    cst_b = din("cst_b", [128, NCB])
    toep = din("toep", [128, 8 * 640])
    out = nc.dram_tensor("out", [S, D], F32, kind="ExternalOutput").ap()
    wsc = nc.dram_tensor("wsc", [NCHUNK, 128, 4096], BF16, kind="Internal").ap()
    glw = nc.dram_tensor("glw", [128, 128], BF16, kind="Internal").ap()
    xv = x.rearrange("(t s p) d -> t p s d", s=2, p=128)
    ov = out.rearrange("(t s p) d -> t p s d", s=2, p=128)

    P = Prog(nc)

    def sb(name, shape, dt):
        return nc.alloc_sbuf_tensor(name, list(shape), dt).ap()

    cf = sb("cf", [128, NCF], F32)
    cb = sb("cb", [128, NCB], BF16)
    toepb = sb("toepb", [128, 8, 640], BF16)
    glwW = sb("glwW", [128, 8, 16], BF16)
    KT = sb("KT", [128, 4, S], BF16)
    V = sb("V", [128, 32, 8, 65], BF16)
    kmT = sb("kmT", [128, 4, 16], BF16)
    ksum = sb("ksum", [128, 4], F32)
    state = sb("state", [128, 4, 256], F32)
    stateb = sb("stateb", [128, 4, 256], BF16)
    xres2 = sb("xres", [128, 2, 2, D], F32)
    nT = sb("nT", [128, 8, T], BF16)
    tokf = sb("tokf", [128, D], F32)
    tokb = tokf.bitcast(BF16)[:, 0:D]
    QT = sb("QT", [128, 8, T], BF16)
    sqb = sb("sqb", [128, 2, T], BF16)
    gqT = sb("gqT", [128, 4, T], F32)
    gkT = sb("gkT", [128, 4, T], F32)
    gktok = sb("gktok", [128, 2, 512], F32)
    kdT = gktok[:, 0, :].bitcast(BF16)[:, 0:512].rearrange("p (h c) -> p h c", h=4)
    gv = sb("gv", [128, 2, D], BF16)
    glowT = sb("glowT", [32, T], F32)
    G2 = sb("G2", [128, 2, D], BF16)
    sigT = sb("sigT", [128, 2, 8, T], BF16)
    gp = sb("gp", [128, 8, 16], F32)
    top8 = sb("top8", [128, 8, 8], F32)
    m01 = sb("m01", [128, 8, 16], F32)
    mb = sb("mb", [128, 8, 16], BF16)
    maskT = sb("maskT", [128, 8, T], BF16)
    PT = [sb(f"PT{i}", [128, 512], BF16) for i in range(3)]
    recip2 = [sb(f"recip{i}", [128, T], F32) for i in range(2)]
    yAT8 = sb("yAT8", [128, 8, T], BF16)
    spb = sb("spb", [128, 512], F32)
    ebT = sb("ebT", [128, 4, 128], F32)
    einvT = sb("einvT", [128, 4, 128], F32)
    erb = sb("erb", [128, 512], F32)
    qdT = sb("qdT", [128, 4, 128], BF16)
    kiT = sb("kiT", [128, 4, 128], BF16)
    kdec = sb("kdec", [128, 512], BF16)
    attm = sb("attm", [128, 4, 128], BF16)
    small = sb("small", [128, 32], F32)
    f5 = [sb(f"f5_{i}", [128, 512], F32) for i in range(4)]
    ring = sb("ring", [128, NB, 4096], BF16)
    ps = [nc.alloc_psum_tensor(f"ps{i}", [128, 512], F32).ap() for i in range(8)]

    def C(name, rows=128):
        o, w = CF[name]
        return cf[0:rows, o:o + w]

    def CBv(name, rows=128):
        o, w = CB[name]
        return cb[0:rows, o:o + w]

    identb = CBv("identb")
    blockones = CBv("blockones")
    blockind = CBv("blockind", 128)
    identf = C("identf")

    st = dict(rot=0, f5=0, pt=0, at=0)

    def rot():
        i = st["rot"] % 4
        st["rot"] += 1
        return ps[i], f"ps{i}"

    def rotS():
        i = st.get("rotS", 0) % 3
        st["rotS"] = st.get("rotS", 0) + 1
        return ps[i], f"ps{i}"

    def rotM():
        return ps[3], "ps3"

    def f5n():
        i = st["f5"] % 4
        st["f5"] += 1
        return f5[i], f"f5_{i}"

    dbg_outs = {}

    def dump(name, ap, tile, reads):
        if name in dbg and tile == dbg_tile:
            shape = list(ap.shape)
            dt = nc.dram_tensor("dbg_" + name, shape, ap.dtype, kind="ExternalOutput").ap()
            dbg_outs[name] = (shape, ap.dtype)
            P.dma("sp", dt, ap, sem="dbg_" + name, reads=reads, writes=["dbgd_" + name])
            P.finish("sp", ["dbgd_" + name])

    cast_fns = {}

    def cast(dst, src, k):
        cast_fns[k] = lambda: P.dma("pool", dst, src, sem=f"cast{k}", reads=[], writes=[f"wsc{k}"])

    cast_state = dict(next=0)

    def emit_casts_upto(k):
        while cast_state["next"] <= min(k, NCHUNK - 1):
            cast_fns[cast_state["next"]]()
            cast_state["next"] += 1

    w_in_v = w_in.rearrange("(c p) j -> p c j", p=128)
    P.dma("pool", glw.rearrange("p (c j) -> p c j", c=8), w_in_v[:, :, GLOW_COL:GLOW_COL + 16], sem="castg",
          reads=[], writes=["glw"])
    for k in range(13):
        c0 = WIN_COLS[k]
        cast(wsc[k].rearrange("p (c j) -> p c j", c=8), w_in_v[:, :, c0:c0 + 512], k)
    w_bm_v = w_bm.rearrange("(h d) j -> d h j", d=64)
    w_bg_v = w_bg.rearrange("(c p) j -> p c j", p=128)
    w_out_v = w_out.rearrange("(c p) j -> p c j", p=128)
    w_up_v = w_up.rearrange("(c p) f -> p c f", p=128)
    w_dn_v = w_down.rearrange("(j p) m -> p j m", p=128)
    for hf in range(2):
        cast(wsc[13 + hf, 0:64, :].rearrange("p (h j) -> p h j", h=8), w_bm_v[:, :, hf * 512:(hf + 1) * 512],
             13 + hf)
        cast(wsc[15 + hf].rearrange("p (c j) -> p c j", c=8), w_bg_v[:, :, hf * 512:(hf + 1) * 512], 15 + hf)
    for hf in range(2):
        cast(wsc[17 + hf].rearrange("p (c j) -> p c j", c=8), w_out_v[:, :, hf * 512:(hf + 1) * 512], 17 + hf)
    for g in range(8):
        cast(wsc[19 + 2 * g].rearrange("p (c j) -> p c j", c=8), w_up_v[:, :, g * 512:(g + 1) * 512], 19 + 2 * g)
        cast(wsc[20 + 2 * g].rearrange("p (j m) -> p j m", j=4), w_dn_v[:, 4 * g:4 * g + 4, :], 20 + 2 * g)

    emit_casts_upto(2)
    P.dma("sp", cf, cst_f, sem="c_cf", reads=[], writes=["cf"])
    P.dma("pool", cb, cst_b, sem="c_cb", reads=[], writes=["cb"])
    P.dma("pool", toepb.rearrange("p h j -> p (h j)"), toep, sem="c_toep", reads=[], writes=["toepb"])
    P.dma("sp", glwW.rearrange("p c j -> p (c j)"), glw, sem="c_glw", reads=["glw"], writes=["glwW"])
    P.op("dve", lambda e: e.memset(V.rearrange("p a h d -> p (a h d)"), 1.0), writes=[f"V{i}" for i in range(NTILE)])
    P.op("pool", lambda e: e.memset(glowT, 1.0), writes=["glowT"])
    P.op("pool", lambda e: e.memset(QT.rearrange("p h t -> p (h t)"), 0.0), writes=[f"QT{i}" for i in range(8)])
    P.op("pool", lambda e: e.memset(maskT.rearrange("p h t -> p (h t)"), 0.0), writes=["maskT"])
    for _i in range(2):
        P.op("pool", lambda e, _i=_i: e.memset(recip2[_i], 0.0), writes=[f"recip{_i}"])
    P.op("pool", lambda e: e.memset(yAT8.rearrange("p h t -> p (h t)"), 0.0), writes=["yAT8"])
    P.op("pool", lambda e: e.memset(state.rearrange("p h v -> p (h v)"), 0.0), writes=[f"state{i}" for i in range(4)])
    P.op("pool", lambda e: e.memset(stateb.rearrange("p h v -> p (h v)"), 0.0), writes=[f"stateb{i}" for i in range(4)])
    P.op("pool", lambda e: e.memset(kmT.rearrange("p a n -> p (a n)"), 0.0), writes=[f"kmT{i}" for i in range(4)])

    rg = dict(next_load=0, cur=0)
    total_chunks = ntiles * NCHUNK

    def ring_load():
        j = rg["next_load"]
        if j >= total_chunks:
            return
        emit_casts_upto(j + 3)
        rg["next_load"] += 1
        k = j % NCHUNK
        slot = j % NB
        if k in (13, 14):
            P.dma("sp", ring[0:64, slot, :], wsc[k, 0:64, :], sem=f"ring{slot}", reads=[f"wsc{k}"],
                  writes=[f"ring{slot}"])
        else:
            P.dma("sp", ring[:, slot, :], wsc[k], sem=f"ring{slot}", reads=[f"wsc{k}"], writes=[f"ring{slot}"])

    def ring_get(expect_k, ahead=0):
        j = rg["cur"] + ahead
        assert j % NCHUNK == expect_k, (j, expect_k)
        assert j < rg["next_load"], "chunk not yet scheduled"
        slot = j % NB
        return ring[:, slot, :], f"ring{slot}"

    def ring_done():
        rg["cur"] += 1
        ring_load()

    P.dma("sp", xres2[:, 0], xv[0], sem="xld0", reads=[], writes=["xres0"])
    for _ in range(NB):
        ring_load()

    def act(out_, in_, func, R, W, bias=0.0, scale=1.0, accum=None):
        if accum is None:
            P.op("act", lambda e: e.activation(out=out_, in_=in_, func=func, bias=bias, scale=scale), R, W)
        else:
            P.op("act", lambda e: e.activation(out=out_, in_=in_, func=func, bias=bias, scale=scale,
                                               accum_out=accum), R, W)

    def mm(out_, lhsT, rhs, start, stop, R, W):
        P.op("pe", lambda e: e.matmul(out_, lhsT=lhsT, rhs=rhs, start=start, stop=stop), R, W)

    def tr(out_, in_, ident, R, W):
        P.op("pe", lambda e: e.transpose(out=out_, in_=in_, identity=ident), R, W)

    def tt(eng, out_, in0, in1, op, R, W):
        P.op(eng, lambda e: e.tensor_tensor(out=out_, in0=in0, in1=in1, op=op), R, W)

    def ts(eng, out_, in0, s1, s2, op0, op1, R, W):
        P.op(eng, lambda e: e.tensor_scalar(out=out_, in0=in0, scalar1=s1, scalar2=s2, op0=op0, op1=op1), R, W)

    def stt(out_, in0, scalar, in1, op0, op1, R, W, accum=None):
        if accum is None:
            P.op("dve", lambda e: e.scalar_tensor_tensor(out=out_, in0=in0, scalar=scalar, in1=in1, op0=op0, op1=op1),
                 R, W)
        else:
            P.op("dve", lambda e: e.scalar_tensor_tensor(out=out_, in0=in0, scalar=scalar, in1=in1, op0=op0, op1=op1,
                                                         accum_out=accum), R, W)

    tokb2 = [tokf.bitcast(BF16)[:, 0:D], tokf.bitcast(BF16)[:, D:2 * D]]

    def norm_stats(xres, xrn):
        for s in range(2):
            junk, jn = f5n()
            act(junk.bitcast(BF16)[:, 0:D], xres[:, s, :], AF.Square, [xrn], [jn, f"small_ss{s}"],
                accum=small[:, 20 + s:21 + s])
        act(small[:, 22:24], small[:, 20:22], AF.Ln, ["small_ss0", "small_ss1"], ["small_ln"], bias=EPS, scale=1.0 / D)
        act(small[:, 24:26], small[:, 22:24], AF.Exp, ["small_ln"], ["small_rs"], scale=-0.5)
        for s in range(2):
            ts("dve", tokb2[s], xres[:, s, :], small[:, 24 + s:25 + s], None, ALU.mult, ALU.bypass,
               [xrn, "small_rs"], [f"tok{s}"])

    def norm_T(gname):
        gT = C(gname)
        for s in range(2):
            for half in range(2):
                pb, pn = rot()
                pbb = pb.bitcast(BF16)
                for c4 in range(4):
                    c = half * 4 + c4
                    tr(pbb[:, c4 * 128:(c4 + 1) * 128], tokb2[s][:, c * 128:(c + 1) * 128], identb,
                       [f"tok{s}", "cb"], [pn])
                tt("dve", nT[:, half * 4:(half + 1) * 4, s * 128:(s + 1) * 128],
                   pbb[:, 0:512].rearrange("p (c t) -> p c t", c=4),
                   gT[:, half * 4:(half + 1) * 4].unsqueeze(2).to_broadcast([128, 4, 128]), ALU.mult,
                   [pn, "cf"], ["nT"])

    def rms_norm_to_T(gname, t, xres, xrn):
        norm_stats(xres, xrn)
        norm_T(gname)

    def proj_fm(w, wn, col0, ncols, out_ps, pn, rhs_cols=None):
        w3 = w.rearrange("p (c j) -> p c j", c=8)
        for c in range(8):
            mm(out_ps, w3[:, c, col0:col0 + ncols], nT[:, c, :], c == 0, c == 7, [wn, "nT"], [pn])

    def proj_tm(w, wn, s, out_ps, pn):
        w3 = w.rearrange("p (c j) -> p c j", c=8)
        for c in range(8):
            mm(out_ps, nT[:, c, s * 128:(s + 1) * 128], w3[:, c, :], c == 0, c == 7, [wn, "nT"], [pn])

    def stage_pt(k, t):
        if stage <= k:
            P.dma("sp", ov[t], xres, sem=f"ost{t % 2}", reads=[xrn], writes=[f"out{t}"])
            raise _Stop()

    try:
      for t in range(ntiles):
        xres = xres2[:, t % 2]
        xrn = f"xres{t % 2}"
        if t + 1 < ntiles:
            P.dma("sp", xres2[:, (t + 1) % 2], xv[t + 1], sem=f"xld{(t + 1) % 2}", reads=[],
                  writes=[f"xres{(t + 1) % 2}"])
        stage_pt(0, t)
        if t == 0 or stage < 99:
            rms_norm_to_T("g1T", t, xres, xrn)
        stage_pt(1, t)
        dump("nT", nT, t, ["nT"])

        pb, pn = rot()
        for c in range(8):
            mm(pb[0:16, 0:T], glwW[:, c, :], nT[:, c, :], c == 0, c == 7, ["glwW", "nT"], [pn])
        P.op("dve", lambda e, pb=pb: e.tensor_copy(out=glowT[0:16, :], in_=pb[0:16, 0:T]), [pn], ["glowT"])

        stage_pt(1.1, t)
        def qk_finish(p):
            which, hp, qf, qfn, slot = p
            gcol = C("gqc") if which == 0 else C("gkc")
            pb2, pn2 = rot()
            mm(pb2[:, 0:T], blockones, sq3[slot][0], True, True, ["cb", sq3[slot][1]], [pn2])
            act(qf[:, T:2 * T], pb2[:, 0:T], AF.Ln, [pn2], [qfn + "b"], bias=EPS, scale=1.0 / 64)
            if which == 0:
                act(qf[:, T:2 * T], qf[:, T:2 * T], AF.Exp, [qfn + "b"], [qfn + "b"], scale=-0.5,
                    bias=float(np.log(0.125)))
                for par in range(2):
                    r0 = 64 * par
                    stt(QT[r0:r0 + 64, 2 * hp + par, :], qf[r0:r0 + 64, 0:T], gcol[r0:r0 + 64, :],
                        qf[r0:r0 + 64, T:2 * T], ALU.mult, ALU.mult, [qfn, qfn + "b", "cf"], [f"QT{2 * hp + par}"])
            else:
                act(qf[:, T:2 * T], qf[:, T:2 * T], AF.Exp, [qfn + "b"], [qfn + "b"], scale=-0.5)
                stt(KT[:, hp, t * T:(t + 1) * T], qf[:, 0:T], gcol, qf[:, T:2 * T], ALU.mult, ALU.mult,
                    [qfn, qfn + "b", "cf"], [f"KT{hp}_{t}", "ksum"], accum=ksum[:, hp:hp + 1])
                ts("dve", kmT[:, hp, t:t + 1], ksum[:, hp:hp + 1], 1.0 / 256, None, ALU.mult, ALU.bypass,
                   ["ksum"], [f"kmT{hp}"])

        pend = []
        sq3 = [(sqb[:, 0, :], "sqb0"), (sqb[:, 1, :], "sqb1"), (PT[2][:, 0:T], "PT2")]
        for which in range(2):
            w, wn = ring_get(which)
            for hp in range(4):
                pb, pn = rot()
                proj_fm(w, wn, hp * 128, 128, pb[:, 0:T], pn)
                qf, qfn = f5n()
                slot = (which * 4 + hp) % 3
                P.op("dve", lambda e, qf=qf, pb=pb: e.tensor_copy(out=qf[:, 0:T], in_=pb[:, 0:T]), [pn], [qfn])
                tt("pool", sq3[slot][0], qf[:, 0:T], qf[:, 0:T], ALU.mult, [qfn], [sq3[slot][1]])
                if len(pend) == 2:
                    qk_finish(pend.pop(0))
                pend.append((which, hp, qf, qfn, slot))
            ring_done()
        while pend:
            qk_finish(pend.pop(0))
        dump("QT", QT, t, [f"QT{i}" for i in range(8)])
        dump("KT", KT[:, :, t * T:(t + 1) * T], t, [f"KT{i}_{t}" for i in range(4)])

        stage_pt(1.2, t)
        w, wn = ring_get(2)
        for s in range(2):
            pb, pn = rot()
            proj_tm(w, wn, s, pb, pn)
            act(V[:, 2 * t + s, :, 0:64], pb.rearrange("p (h d) -> p h d", h=8), AF.Copy, [pn], [f"V{t}"])
        ring_done()

        stage_pt(1.3, t)
        w, wn = ring_get(3)
        for h in range(4):
            pb, pn = rot()
            proj_fm(w, wn, h * 128, 128, pb[:, 0:T], pn)
            act(gqT[:, h, :], pb[:, 0:T], AF.Copy, [pn], ["gqT"])
        ring_done()
        w, wn = ring_get(4)
        for h in range(4):
            pb, pn = rot()
            proj_fm(w, wn, h * 128, 128, pb[:, 0:T], pn)
            P.op("dve", lambda e, pb=pb, h=h: e.tensor_copy(out=gkT[:, h, :], in_=pb[:, 0:T]), [pn], ["gkT"])
        ring_done()
        for half in range(2):
            w, wn = ring_get(5 + half)
            for s in range(2):
                pb, pn = rot()
                proj_tm(w, wn, s, pb, pn)
                P.op("dve", lambda e, pb=pb, s=s, half=half: e.tensor_copy(
                    out=gv[:, s, half * 512:(half + 1) * 512], in_=pb), [pn], ["gv"])
            ring_done()
        stage_pt(1.4, t)
        for half in range(2):
            w, wn = ring_get(7 + half)
            for s in range(2):
                pb, pn = rot()
                proj_tm(w, wn, s, pb, pn)
                tm, tmn = f5n()
                act(tm, pb, AF.Sigmoid, [pn], [tmn])
                tt("dve", G2[:, s, half * 512:(half + 1) * 512], pb, tm, ALU.mult, [pn, tmn], ["G2"])
            ring_done()
        stage_pt(1.5, t)
        def gla_gen():
            for s in range(2):
                sl = slice(s * 128, (s + 1) * 128)
                pb, pn = rotM()
                mm(pb, glowT[:, sl], C("gup", 32), True, True, ["glowT", "cf"], [pn])
                act(erb, pb, AF.Exp, [pn], ["erb"], scale=-1.0)
                yield
                act(spb, erb, AF.Ln, ["erb"], ["spb"], bias=1.0)
                yield
                pbT, pnT = rotM()
                for h in range(4):
                    mm(pbT[:, h * 128:(h + 1) * 128], spb[:, h * 128:(h + 1) * 128], C("tri_incl"), True, True,
                       ["spb", "cf"], [pnT])
                act(ebT.rearrange("p h c -> p (h c)"), pbT, AF.Exp, [pnT], ["ebT"])
                act(einvT.rearrange("p h c -> p (h c)"), pbT, AF.Exp, [pnT], ["einvT"], scale=-1.0)
                yield
                stt(qdT, gqT[:, :, sl], float(128 ** -0.5), ebT, ALU.mult, ALU.mult, ["gqT", "ebT"], ["qdT"])
                tt("dve", kiT, gkT[:, :, sl], einvT, ALU.mult, ["gkT", "einvT"], ["kiT"])
                yield
                for h in range(4):
                    ts("dve", kdT[:, h, :], kiT[:, h, :], ebT[:, h, 127:128], None, ALU.mult, ALU.bypass,
                       ["kiT", "ebT"], ["gktok"])
                yield
                pbk, pnk = rotM()
                pbkb = pbk.bitcast(BF16)
                for h in range(4):
                    tr(pbkb[:, h * 128:(h + 1) * 128], kdT[:, h, :], identb, ["gktok", "cb"], [pnk])
                P.op("dve", lambda e, pbkb=pbkb: e.tensor_copy(out=kdec, in_=pbkb[:, 0:512]), [pnk], ["kdec"])
                yield
                pba, pna = rotM()
                for h in range(4):
                    mm(pba[:, h * 128:(h + 1) * 128], kiT[:, h, :], qdT[:, h, :], True, True, ["kiT", "qdT"], [pna])
                tt("dve", attm, pba.rearrange("p (h c) -> p h c", h=4),
                   C("causal").unsqueeze(1).to_broadcast([128, 4, 128]), ALU.mult, [pna, "cf"], ["attm"])
                yield
                for hh in range(2):
                    for h in (2 * hh, 2 * hh + 1):
                        reg = ps[6][:, (h % 2) * 256:(h % 2 + 1) * 256]
                        mm(reg, attm[:, h, :], gv[:, s, h * 256:(h + 1) * 256], True, False, ["attm", "gv"], ["ps6"])
                        mm(reg, qdT[:, h, :], stateb[:, h, :], False, True, ["qdT", f"stateb{h}"], ["ps6"])
                    for h in (2 * hh, 2 * hh + 1):
                        mm(ps[7][:, (h % 2) * 256:(h % 2 + 1) * 256], kdec[:, h * 128:(h + 1) * 128],
                           gv[:, s, h * 256:(h + 1) * 256], True, True, ["kdec", "gv"], ["ps7"])
                    yield
                    for h in (2 * hh, 2 * hh + 1):
                        stt(state[:, h, :], state[:, h, :], ebT[:, h, 127:128], ps[7][:, (h % 2) * 256:(h % 2 + 1) * 256],
                            ALU.mult, ALU.add, [f"state{h}", "ebT", "ps7"], [f"state{h}"])
                        P.op("pool", lambda e, h=h: e.tensor_copy(out=stateb[:, h, :], in_=state[:, h, :]),
                             [f"state{h}"], [f"stateb{h}"])
                    sqt, sqtn = f5n()
                    for h in (2 * hh, 2 * hh + 1):
                        reg = ps[6][:, (h % 2) * 256:(h % 2 + 1) * 256]
                        act(sqt[:, (h % 2) * 256:(h % 2 + 1) * 256], reg, AF.Square, ["ps6"], [sqtn, f"small_o{h}"],
                            accum=small[:, 8 + h:9 + h])
                    yield
                    act(small[:, 12 + 2 * hh:14 + 2 * hh], small[:, 8 + 2 * hh:10 + 2 * hh], AF.Ln,
                        [f"small_o{2 * hh}", f"small_o{2 * hh + 1}"], [f"small_ol{hh}"], bias=EPS, scale=1.0 / 256)
                    yield
                    act(small[:, 16 + 2 * hh:18 + 2 * hh], small[:, 12 + 2 * hh:14 + 2 * hh], AF.Exp,
                        [f"small_ol{hh}"], [f"small_or{hh}"], scale=-0.5)
                    yield
                    for h in (2 * hh, 2 * hh + 1):
                        reg = ps[6][:, (h % 2) * 256:(h % 2 + 1) * 256]
                        stt(tokb[:, h * 256:(h + 1) * 256], reg, small[:, 16 + h:17 + h], G2[:, s, h * 256:(h + 1) * 256],
                            ALU.mult, ALU.mult, ["ps6", f"small_or{hh}", "G2"], ["tok0"])
                    yield
                for half in range(2):
                    pb, pn = rotM()
                    pbb = pb.bitcast(BF16)
                    for c4 in range(4):
                        c = half * 4 + c4
                        tr(pbb[:, c4 * 128:(c4 + 1) * 128], tokb[:, c * 128:(c + 1) * 128], identb, ["tok0", "cb"], [pn])
                    tt("dve", nT[:, half * 4:(half + 1) * 4, sl], pbb[:, 0:512].rearrange("p (c t) -> p c t", c=4),
                       C("gonT")[:, half * 4:(half + 1) * 4].unsqueeze(2).to_broadcast([128, 4, 128]), ALU.mult,
                       [pn, "cf"], ["nT"])
                    yield

        gla = gla_gen() if stage > 3 else iter(())
        for which in range(2):
            for half in range(2):
                w, wn = ring_get(9 + 2 * which + half)
                for cc in range(4):
                    c = half * 4 + cc
                    pb, pn = rot()
                    proj_fm(w, wn, cc * 128, 128, pb[:, 0:T], pn)
                    act(sigT[:, which, c, :], pb[:, 0:T], AF.Sigmoid, [pn], ["sigT"])
                ring_done()
        dump("sigT", sigT, t, ["sigT"])
        dump("G2", G2, t, ["G2"])
        dump("gqT", gqT, t, ["gqT"])

        stage_pt(2, t)
        ob = t
        use_mask = ob >= 4
        if use_mask:
            vbw = C("vbvec")[:, 15 - ob:31 - ob]
            for s in range(2):
                pbs = [rot(), rot()]
                for h in range(8):
                    b0 = (h % 2) * 64
                    pb, pn = pbs[h % 2]
                    mm(pb[:, (h // 2) * 16:(h // 2 + 1) * 16], QT[:, h, s * 128:(s + 1) * 128],
                       kmT[:, h // 2, :], True, True, [f"QT{h}", f"kmT{h // 2}"], [pn])
                gp4 = gp.rearrange("p (a b) n -> p a b n", b=2)
                for par in range(2):
                    pb, pn = pbs[par]
                    tt("dve", gp4[:, :, par, :], pb[:, 0:64].rearrange("p (h n) -> p h n", h=4),
                       vbw.unsqueeze(1).to_broadcast([128, 4, 16]), ALU.add, [pn, "cf"], ["gp"])
                for h in range(8):
                    P.op("dve", lambda e, h=h: e.max(out=top8[:, h, :], in_=gp[:, h, :]), ["gp"], ["top8"])
                tt("dve", m01, gp, top8[:, :, 3:4].to_broadcast([128, 8, 16]), ALU.is_ge, ["gp", "top8"], ["m01"])
                ts("dve", mb, m01, 30000.0, -30000.0, ALU.mult, ALU.add, ["m01"], ["mb"])
                pb2, pn2 = rot()
                pbb = pb2.bitcast(BF16)
                for h in range(8):
                    tr(pbb[0:16, h * 128:(h + 1) * 128], mb[:, h, :], identb, ["mb", "cb"], [pn2])
                P.op("dve", lambda e, pbb=pbb, s=s: e.tensor_copy(
                    out=maskT[0:16, :, s * 128:(s + 1) * 128], in_=pbb[0:16, :].rearrange("p (h q) -> p h q", h=8)),
                    [pn2], ["maskT"])
            dump("maskT", maskT, t, ["maskT"])

        items = [(h, n) for h in range(8) for n in range(ob + 1)]

        def stA(h, n):
            hp = h // 2
            sbk, sn = rotS()
            for kt in range(2):
                ktile = 2 * n + kt
                reg = sbk[:, kt * T:(kt + 1) * T]
                extra = (n < ob and use_mask) or (n >= ob - 1)
                mm(reg, KT[:, hp, ktile * 128:(ktile + 1) * 128], QT[:, h, :], True,
                   not extra, [f"KT{hp}_{n}", f"QT{h}"], [sn])
                if n < ob and use_mask:
                    last = not (n >= ob - 1)
                    mm(reg, blockind[:, n * 128:(n + 1) * 128], maskT[:, h, :], False, last, ["cb", "maskT"], [sn])
                if n == ob:
                    j0 = 128 - kt * 128
                    mm(reg, identb, toepb[:, h, j0:j0 + T], False, True, ["cb", "toepb"], [sn])
                elif n == ob - 1:
                    j0 = 128 + 256 - kt * 128
                    mm(reg, identb, toepb[:, h, j0:j0 + T], False, True, ["cb", "toepb"], [sn])
            return sbk, sn

        def stB(h, n, sbk, sn):
            pt = PT[st["pt"] % 3]
            ptn = f"PT{st['pt'] % 3}"
            st["pt"] += 1
            if n <= ob - 2:
                act(pt, sbk, AF.Exp, [sn, "cf"], [ptn], bias=C("t5far")[:, h:h + 1])
            else:
                act(pt, sbk, AF.Exp, [sn], [ptn])
            return pt, ptn

        def stC(h, n, pt, ptn):
            if n == 0:
                norm_flush(h % 2)
            oi = 4 + (h % 2)
            Oacc = ps[oi]
            on = f"ps{oi}"
            for kt in range(2):
                ktile = 2 * n + kt
                mm(Oacc[0:65, 0:T], V[:, ktile, h, :], pt[:, kt * T:(kt + 1) * T],
                   n == 0 and kt == 0, n == ob and kt == 1, [f"V{n}", ptn], [on])
            if n == ob:
                norm_q[h % 2] = norm_gen(h)

        def norm_gen(h):
            oi = 4 + (h % 2)
            Oacc = ps[oi]
            on = f"ps{oi}"
            recip = recip2[h % 2]
            rcn = f"recip{h % 2}"
            act(recip[64:65, :], Oacc[64:65, 0:T], AF.Ln, [on], [rcn])
            yield
            act(recip[64:65, :], recip[64:65, :], AF.Exp, [rcn], [rcn], scale=-1.0)
            yield
            mm(Oacc[0:64, T:2 * T], C("ones"), recip, True, True, ["cf", rcn], [on])
            yield
            yield
            bc, bcn = f5n()
            P.op("dve", lambda e: e.tensor_copy(out=bc[0:64, 0:T], in_=Oacc[0:64, T:2 * T]), [on], [bcn])
            tt("dve", yAT8[0:64, h, :], Oacc[0:64, 0:T], bc[0:64, 0:T], ALU.mult, [on, bcn], ["yAT8"])

        norm_q = {0: None, 1: None}

        def norm_flush(par):
            g = norm_q[par]
            if g is not None:
                for _ in g:
                    pass
                norm_q[par] = None

        def norm_step():
            for par in (0, 1):
                g = norm_q[par]
                if g is not None:
                    try:
                        next(g)
                    except StopIteration:
                        norm_q[par] = None

        LOOK = 2
        gla_every = max(1, len(items) // 44)
        Ares = {}
        for i in range(len(items) + LOOK):
            if i < len(items):
                Ares[i] = stA(*items[i])
            j = i - LOOK
            if j >= 0:
                pt, ptn = stB(*items[j], *Ares.pop(j))
                norm_step()
                stC(*items[j], pt, ptn)
            if i % gla_every == 0:
                next(gla, None)
        norm_flush(0)
        norm_flush(1)
        for _ in gla:
            pass
        dump("yAT8", yAT8, t, ["yAT8"])
        dump("yGT", nT, t, ["nT"])
        dump("state", state, t, [f"state{h}" for h in range(4)])

        stage_pt(4, t)
        mixedT = gqT.rearrange("p a t -> p (a t)").bitcast(BF16).rearrange("p (c t) -> p c t", c=8)
        for hf in range(2):
            wA, wAn = ring_get(13 + hf)
            wA3 = wA.rearrange("p (h j) -> p h j", h=8)
            for cc in range(4):
                c = hf * 4 + cc
                pa, pan = rot()
                for h in range(8):
                    mm(pa[:, 0:T], wA3[:, h, cc * 128:(cc + 1) * 128], yAT8[:, h, :], h == 0, h == 7,
                       [wAn, "yAT8"], [pan])
                tt("dve", mixedT[:, c, :], pa[:, 0:T], sigT[:, 0, c, :], ALU.mult, [pan, "sigT"], ["gqT"])
            ring_done()
        for hf in range(2):
            wG, wGn = ring_get(15 + hf)
            wG3 = wG.rearrange("p (c j) -> p c j", c=8)
            for cc in range(4):
                c = hf * 4 + cc
                pg, pgn = rot()
                for j in range(8):
                    mm(pg[:, 0:T], wG3[:, j, cc * 128:(cc + 1) * 128], nT[:, j, :], j == 0, j == 7,
                       [wGn, "nT"], [pgn])
                tm, tmn = f5n()
                tt("dve", tm[:, 0:T], pg[:, 0:T], sigT[:, 1, c, :], ALU.mult, [pgn, "sigT"], [tmn])
                tt("pool", mixedT[:, c, :], tm[:, 0:T], mixedT[:, c, :], ALU.add, [tmn, "gqT"], ["gqT"])
            ring_done()
        dump("mixedT", mixedT, t, ["gqT"])
        for hf in range(2):
            w, wn = ring_get(17 + hf)
            w3 = w.rearrange("p (c j) -> p c j", c=8)
            for s in range(2):
                pb, pn = rot()
                for c in range(8):
                    mm(pb, mixedT[:, c, s * 128:(s + 1) * 128], w3[:, c, :], c == 0, c == 7, [wn, "gqT"], [pn])
                tt("dve", xres[:, s, hf * 512:(hf + 1) * 512], pb, xres[:, s, hf * 512:(hf + 1) * 512], ALU.add,
                   [pn, xrn], [xrn])
            ring_done()
        dump("h1", xres, t, [xrn])

        stage_pt(5, t)
        rms_norm_to_T("g2T", t, xres, xrn)
        stage_pt(5.5, t)
        for g in range(8):
            if g == 1:
                stage_pt(5.7, t)
            if g == 6 and t + 1 < ntiles and stage >= 99:
                norm_stats(xres2[:, (t + 1) % 2], f"xres{(t + 1) % 2}")
            wu, wun = ring_get(19 + 2 * g)
            wu3 = wu.rearrange("p (c j) -> p c j", c=8)
            ats = []
            for j in range(4):
                pb, pn = rot()
                for c in range(8):
                    mm(pb[:, 0:T], wu3[:, c, j * 128:(j + 1) * 128], nT[:, c, :], c == 0, c == 7, [wun, "nT"], [pn])
                a = PT[(st["at"] % 4) // 2][:, (st["at"] % 2) * T:(st["at"] % 2 + 1) * T]
                an = f"PT{(st['at'] % 4) // 2}"
                st["at"] += 1
                rl, rln = f5n()
                act(rl[:, 0:T], pb[:, 0:T], AF.Relu, [pn], [rln])
                tt("pool", a, rl[:, 0:T], rl[:, 0:T], ALU.mult, [rln], [an])
                ats.append((a, an))
            ring_done()
            stage_pt(5.6, t)
            if g == 7 and t + 1 < ntiles and stage >= 99:
                norm_T("g1T")
            wd, wdn = ring_get(20 + 2 * g)
            wd3 = wd.rearrange("p (j m) -> p j m", j=4)
            for j in range(4):
                a, an = ats[j]
                first = (g == 0 and j == 0)
                last = (g == 7 and j == 3)
                for s in range(2):
                    for hf in range(2):
                        bi = 4 + s * 2 + hf
                        mm(ps[bi], a[:, s * 128:(s + 1) * 128], wd3[:, j, hf * 512:(hf + 1) * 512], first, last,
                           [wdn, an], [f"ps{bi}"])
            ring_done()
        for s in range(2):
            for hf in range(2):
                bi = 4 + s * 2 + hf
                tt("dve", xres[:, s, hf * 512:(hf + 1) * 512], ps[bi], xres[:, s, hf * 512:(hf + 1) * 512], ALU.add,
                   [f"ps{bi}", xrn], [xrn])
        P.dma("sp", ov[t], xres, sem=f"ost{t % 2}", reads=[xrn], writes=[f"out{t}"])
    except _Stop:
        pass
    P.finish("sp", [f"out{t}" for t in range(ntiles)])
    P.emit()
    return nc, dbg_outs


def _t5_bucket_np(d):
    d = np.maximum(d, 0)
    large = 16 + (np.log(np.maximum(d, 1).astype(np.float32) / 16) / np.log(128 / 16) * 16).astype(np.int32)
    large = np.minimum(large, 31)
    return np.where(d < 16, d, large)


def host_consts(norm1_g, norm2_g, moba_q_norm_g, moba_k_norm_g, t5_bias, gla_gate_up, gla_gate_bias,
                gla_out_norm_g):
    cf = np.zeros((128, NCF), np.float32)

    def put(name, arr):
        o, w = CF[name]
        cf[:, o:o + w] = arr

    put("g1T", np.asarray(norm1_g, np.float32).reshape(8, 128).T)
    put("g2T", np.asarray(norm2_g, np.float32).reshape(8, 128).T)
    put("gonT", np.tile(np.asarray(gla_out_norm_g, np.float32).reshape(2, 128).T, (1, 4)))
    put("gqc", np.tile(np.asarray(moba_q_norm_g, np.float32).reshape(64), 2).reshape(128, 1))
    put("gkc", np.tile(np.asarray(moba_k_norm_g, np.float32).reshape(64), 2).reshape(128, 1))
    put("t5far", np.broadcast_to(np.asarray(t5_bias, np.float32)[31], (128, 8)))
    vb = np.array([0.0] * 15 + [1e30] + [-1e30] * 15, np.float32)
    put("vbvec", np.broadcast_to(vb, (128, 31)))
    e = np.arange(128)[:, None]
    c = np.arange(128)[None, :]
    put("tri_incl", np.where(e <= c, -1.0 / 16, 0.0).astype(np.float32))
    put("tri_strict", np.where(e > c, -1.0 / 16, 0.0).astype(np.float32))
    put("causal", np.where(c >= e, 1.0, 0.0).astype(np.float32))
    gup = np.zeros((128, 512), np.float32)
    gup[0:16] = np.asarray(gla_gate_up, np.float32).reshape(16, 512)
    gup[16] = np.asarray(gla_gate_bias, np.float32).reshape(512)
    put("gup", gup)
    put("ones", np.ones((128, 64), np.float32))
    put("identf", np.eye(128, dtype=np.float32))

    cbm = np.zeros((128, NCB), np.float32)
    o, w = CB["identb"]
    cbm[:, o:o + w] = np.eye(128, dtype=np.float32)
    o, w = CB["blockones"]
    cbm[:, o:o + w] = (e // 64 == c // 64).astype(np.float32)
    o, w = CB["blockind"]
    bi = np.zeros((128, 2048), np.float32)
    for n in range(16):
        bi[n, n * 128:(n + 1) * 128] = 1.0
    cbm[:, o:o + w] = bi

    t5ext = np.concatenate([np.asarray(t5_bias, np.float32), np.full((1, 8), -30000.0, np.float32)], axis=0)
    k = np.arange(128)[:, None]
    jp = np.arange(640)[None, :]
    d = jp - 128 - k
    idx = np.where(d >= 0, _t5_bucket_np(d), 32)
    toep = np.ascontiguousarray(t5ext[idx].transpose(0, 2, 1)).reshape(128, 8 * 640)
    return cf, cbm, toep


_NC_CACHE = {}


def kernel(x, norm1_g, w_in, moba_q_norm_g, moba_k_norm_g, t5_bias, gla_gate_up, gla_gate_bias,
           gla_out_norm_g, w_branch_moba, w_branch_gla, w_out, norm2_g, w_up, w_down):
    x = np.asarray(x, np.float32)
    B = x.shape[0]
    cf, cbm, toep = host_consts(np.asarray(norm1_g)[0], np.asarray(norm2_g)[0], np.asarray(moba_q_norm_g)[0],
                                np.asarray(moba_k_norm_g)[0], np.asarray(t5_bias), np.asarray(gla_gate_up)[0],
                                np.asarray(gla_gate_bias)[0], np.asarray(gla_out_norm_g)[0])
    shared = {
        "w_in": np.ascontiguousarray(np.asarray(w_in, np.float32)[0]),
        "w_bm": np.ascontiguousarray(np.asarray(w_branch_moba, np.float32)[0]),
        "w_bg": np.ascontiguousarray(np.asarray(w_branch_gla, np.float32)[0]),
        "w_out": np.ascontiguousarray(np.asarray(w_out, np.float32)[0]),
        "w_up": np.ascontiguousarray(np.asarray(w_up, np.float32)[0]),
        "w_down": np.ascontiguousarray(np.asarray(w_down, np.float32)[0]),
        "cst_f": cf, "cst_b": cbm, "toep": toep,
    }
    if "nc" not in _NC_CACHE:
        _NC_CACHE["nc"] = build_nc()[0]
    nc = _NC_CACHE["nc"]
    in_maps = [dict(shared, x=np.ascontiguousarray(x[i])) for i in range(B)]
    res = run_bass_kernel_spmd(nc, in_maps, core_ids=list(range(B)))
    return np.stack([np.asarray(r["out"], np.float32) for r in res.results], axis=0)
```

```python
import numpy as np
import ml_dtypes
import concourse.bass as bass
import concourse.mybir as mybir
from concourse.bass_utils import run_bass_kernel_spmd

F32 = mybir.dt.float32
BF16 = mybir.dt.bfloat16
AF = mybir.ActivationFunctionType
ALU = mybir.AluOpType
AX = mybir.AxisListType


class Prog:
    def __init__(self, nc):
        self.nc = nc
        self.ins = []
        self.dma_cum = {}

    def op(self, eng, fn, reads=(), writes=()):
        writes = list(writes) + [r for r in reads if r.startswith("ps") and r not in writes]
        self.ins.append(dict(eng=eng, fn=fn, reads=list(reads), writes=list(writes), dma=None))

    def dma(self, eng, out, in_, sem, reads=(), writes=(), **kw):
        v = self.dma_cum.get(sem, 0) + 16
        self.dma_cum[sem] = v
        self.ins.append(dict(eng=eng, fn=lambda e: e.dma_start(out=out, in_=in_, **kw),
                             reads=list(reads), writes=list(writes), dma=(sem, v)))

    def finish(self, eng, resources):
        self.ins.append(dict(eng=eng, fn=None, reads=list(resources), writes=[], dma=None))

    def emit(self):
        nc = self.nc
        ins = self.ins
        n = len(ins)
        last_w = {}
        readers = {}
        deps = [None] * n
        needed = [False] * n
        for i, I in enumerate(ins):
            d = set()
            for r in I["reads"]:
                j = last_w.get(r)
                if j is not None:
                    d.add((j, "raw"))
            for r in I["writes"]:
                j = last_w.get(r)
                if j is not None:
                    d.add((j, "waw"))
                for j in readers.get(r, ()):
                    d.add((j, "war"))
            for r in I["reads"]:
                readers.setdefault(r, []).append(i)
            for r in I["writes"]:
                last_w[r] = i
                readers[r] = []
            dd = {}
            for j, kind in d:
                if j == i:
                    continue
                J = ins[j]
                if J["dma"] is None and I["dma"] is None and J["eng"] == I["eng"]:
                    if I["eng"] == "pe" or kind != "raw":
                        continue
                dd[j] = True
            deps[i] = list(dd)
            for j in dd:
                needed[j] = True
        engs = ["pe", "act", "dve", "pool", "sp"]
        sems = {e: nc.alloc_semaphore("sem_" + e) for e in engs}
        dsems = {k: nc.alloc_semaphore("dsem_" + k) for k in self.dma_cum}
        cnt = {e: 0 for e in engs}
        token = [None] * n
        for i, I in enumerate(ins):
            if I["dma"] is not None:
                token[i] = ("d:" + I["dma"][0], I["dma"][1])
            elif I["fn"] is not None and needed[i]:
                cnt[I["eng"]] += 1
                token[i] = ("e:" + I["eng"], cnt[I["eng"]])
        streams = {e: [] for e in engs}
        waited = {e: {} for e in engs}
        dma_issued = {}
        n_wait = 0
        for i, I in enumerate(ins):
            e = I["eng"]
            want = {}
            for j in deps[i]:
                tk = token[j]
                if tk is None:
                    continue
                key, val = tk
                if key.startswith("d:"):
                    val = max(val, dma_issued.get(key, 0))
                if want.get(key, 0) < val:
                    want[key] = val
            for key, val in want.items():
                if waited[e].get(key, 0) >= val:
                    continue
                waited[e][key] = val
                s = dsems[key[2:]] if key.startswith("d:") else sems[key[2:]]
                streams[e].append(("wait", s, val))
                n_wait += 1
            if I["dma"] is not None:
                dma_issued["d:" + I["dma"][0]] = I["dma"][1]
                streams[e].append(("dma", I["fn"], dsems[I["dma"][0]]))
            elif I["fn"] is not None:
                streams[e].append(("op", I["fn"], sems[e] if needed[i] else None))
        self.stats = dict(n_ins=n, n_wait=n_wait, per_eng={e: len(streams[e]) for e in engs})

        def run(e_obj, lst):
            for it in lst:
                if it[0] == "wait":
                    e_obj.wait_ge(it[1], it[2])
                elif it[0] == "dma":
                    it[1](e_obj).then_inc(it[2], 16)
                else:
                    r = it[1](e_obj)
                    if it[2] is not None:
                        r.then_inc(it[2], 1)

        with nc.Block() as block:
            @block.tensor
            def _(e):
                run(e, streams["pe"])

            @block.scalar
            def _(e):
                run(e, streams["act"])

            @block.vector
            def _(e):
                run(e, streams["dve"])

            @block.gpsimd
            def _(e):
                run(e, streams["pool"])

            @block.sync
            def _(e):
                run(e, streams["sp"])


S = 4096
D = 1024
T = 256
NTILE = S // T
DIN = 6672
DFF = 4096
NB = 3
EPS = 1e-6
NCHUNK = 35
WIN_COLS = [0, 512, 1024, 1536, 2048, 2560, 3072, 3600, 4112, 4624, 5136, 5648, 6160]
GLOW_COL = 3584

CF = {}
_o = 0
for _n, _w in [("g1T", 8), ("g2T", 8), ("gonT", 8), ("gqc", 1), ("gkc", 1), ("t5far", 8), ("vbvec", 31),
               ("tri_incl", 128), ("tri_strict", 128), ("causal", 128), ("gup", 512), ("ones", 64),
               ("identf", 128)]:
    CF[_n] = (_o, _w)
    _o += _w
NCF = _o
CB = {}
_o = 0
for _n, _w in [("identb", 128), ("blockones", 128), ("blockind", 2048)]:
    CB[_n] = (_o, _w)
    _o += _w
NCB = _o


class _Stop(Exception):
    pass


def build_nc(ntiles=NTILE, dbg=None, dbg_tile=0, stage=99):
    dbg = dbg or []
    nc = bass.Bass("TRN2", target_bir_lowering=False)

    def din(name, shape, dt=F32):
        return nc.dram_tensor(name, list(shape), dt, kind="ExternalInput").ap()

    x = din("x", [S, D])
    w_in = din("w_in", [D, DIN])
    w_bm = din("w_bm", [512, D])
    w_bg = din("w_bg", [D, D])
    w_out = din("w_out", [D, D])
    w_up = din("w_up", [D, DFF])
    w_down = din("w_down", [DFF, D])
    cst_f = din("cst_f", [128, NCF])
    cst_b = din("cst_b", [128, NCB])
    toep = din("toep", [128, 8 * 640])
    out = nc.dram_tensor("out", [S, D], F32, kind="ExternalOutput").ap()
    wsc = nc.dram_tensor("wsc", [NCHUNK, 128, 4096], BF16, kind="Internal").ap()
    glw = nc.dram_tensor("glw", [128, 128], BF16, kind="Internal").ap()
    xv = x.rearrange("(t s p) d -> t p s d", s=2, p=128)
    ov = out.rearrange("(t s p) d -> t p s d", s=2, p=128)

    P = Prog(nc)

    def sb(name, shape, dt):
        return nc.alloc_sbuf_tensor(name, list(shape), dt).ap()

    cf = sb("cf", [128, NCF], F32)
    cb = sb("cb", [128, NCB], BF16)
    toepb = sb("toepb", [128, 8, 640], BF16)
    glwW = sb("glwW", [128, 8, 16], BF16)
    KT = sb("KT", [128, 4, S], BF16)
    V = sb("V", [128, 32, 8, 65], BF16)
    kmT = sb("kmT", [128, 4, 16], BF16)
    ksum = sb("ksum", [128, 4], F32)
    state = sb("state", [128, 4, 256], F32)
    stateb = sb("stateb", [128, 4, 256], BF16)
    xres2 = sb("xres", [128, 2, 2, D], F32)
    nT = sb("nT", [128, 8, T], BF16)
    tokf = sb("tokf", [128, D], F32)
    tokb = tokf.bitcast(BF16)[:, 0:D]
    QT = sb("QT", [128, 8, T], BF16)
    sqb = sb("sqb", [128, 2, T], BF16)
    gqT = sb("gqT", [128, 4, T], F32)
    gkT = sb("gkT", [128, 4, T], F32)
    gktok = sb("gktok", [128, 2, 512], F32)
    kdT = gktok[:, 0, :].bitcast(BF16)[:, 0:512].rearrange("p (h c) -> p h c", h=4)
    gv = sb("gv", [128, 2, D], BF16)
    glowT = sb("glowT", [32, T], F32)
    G2 = sb("G2", [128, 2, D], BF16)
    sigT = sb("sigT", [128, 2, 8, T], BF16)
    gp = sb("gp", [128, 8, 16], F32)
    top8 = sb("top8", [128, 8, 8], F32)
    m01 = sb("m01", [128, 8, 16], F32)
    mb = sb("mb", [128, 8, 16], BF16)
    maskT = sb("maskT", [128, 8, T], BF16)
    PT = [sb(f"PT{i}", [128, 512], BF16) for i in range(3)]
    recip2 = [sb(f"recip{i}", [128, T], F32) for i in range(2)]
    yAT8 = sb("yAT8", [128, 8, T], BF16)
    spb = sb("spb", [128, 512], F32)
    ebT = sb("ebT", [128, 4, 128], F32)
    einvT = sb("einvT", [128, 4, 128], F32)
    erb = sb("erb", [128, 512], F32)
    qdT = sb("qdT", [128, 4, 128], BF16)
    kiT = sb("kiT", [128, 4, 128], BF16)
    kdec = sb("kdec", [128, 512], BF16)
    attm = sb("attm", [128, 4, 128], BF16)
    small = sb("small", [128, 32], F32)
    f5 = [sb(f"f5_{i}", [128, 512], F32) for i in range(4)]
    ring = sb("ring", [128, NB, 4096], BF16)
    ps = [nc.alloc_psum_tensor(f"ps{i}", [128, 512], F32).ap() for i in range(8)]

    def C(name, rows=128):
        o, w = CF[name]
        return cf[0:rows, o:o + w]

    def CBv(name, rows=128):
        o, w = CB[name]
        return cb[0:rows, o:o + w]

    identb = CBv("identb")
    blockones = CBv("blockones")
    blockind = CBv("blockind", 128)
    identf = C("identf")

    st = dict(rot=0, f5=0, pt=0, at=0)

    def rot():
        i = st["rot"] % 4
        st["rot"] += 1
        return ps[i], f"ps{i}"

    def rotS():
        i = st.get("rotS", 0) % 3
        st["rotS"] = st.get("rotS", 0) + 1
        return ps[i], f"ps{i}"

    def rotM():
        return ps[3], "ps3"

    def f5n():
        i = st["f5"] % 4
        st["f5"] += 1
        return f5[i], f"f5_{i}"

    dbg_outs = {}

    def dump(name, ap, tile, reads):
        if name in dbg and tile == dbg_tile:
            shape = list(ap.shape)
            dt = nc.dram_tensor("dbg_" + name, shape, ap.dtype, kind="ExternalOutput").ap()
            dbg_outs[name] = (shape, ap.dtype)
            P.dma("sp", dt, ap, sem="dbg_" + name, reads=reads, writes=["dbgd_" + name])
            P.finish("sp", ["dbgd_" + name])

    cast_fns = {}

    def cast(dst, src, k):
        cast_fns[k] = lambda: P.dma("pool", dst, src, sem=f"cast{k}", reads=[], writes=[f"wsc{k}"])

    cast_state = dict(next=0)

    def emit_casts_upto(k):
        while cast_state["next"] <= min(k, NCHUNK - 1):
            cast_fns[cast_state["next"]]()
            cast_state["next"] += 1

    w_in_v = w_in.rearrange("(c p) j -> p c j", p=128)
    P.dma("pool", glw.rearrange("p (c j) -> p c j", c=8), w_in_v[:, :, GLOW_COL:GLOW_COL + 16], sem="castg",
          reads=[], writes=["glw"])
    for k in range(13):
        c0 = WIN_COLS[k]
        cast(wsc[k].rearrange("p (c j) -> p c j", c=8), w_in_v[:, :, c0:c0 + 512], k)
    w_bm_v = w_bm.rearrange("(h d) j -> d h j", d=64)
    w_bg_v = w_bg.rearrange("(c p) j -> p c j", p=128)
    w_out_v = w_out.rearrange("(c p) j -> p c j", p=128)
    w_up_v = w_up.rearrange("(c p) f -> p c f", p=128)
    w_dn_v = w_down.rearrange("(j p) m -> p j m", p=128)
    for hf in range(2):
        cast(wsc[15 + hf, 0:64, :].rearrange("p (h j) -> p h j", h=8), w_bm_v[:, :, hf * 512:(hf + 1) * 512],
             15 + hf)
        cast(wsc[13 + hf].rearrange("p (c j) -> p c j", c=8), w_bg_v[:, :, hf * 512:(hf + 1) * 512], 13 + hf)
    for hf in range(2):
        cast(wsc[17 + hf].rearrange("p (c j) -> p c j", c=8), w_out_v[:, :, hf * 512:(hf + 1) * 512], 17 + hf)
    for g in range(8):
        cast(wsc[19 + 2 * g].rearrange("p (c j) -> p c j", c=8), w_up_v[:, :, g * 512:(g + 1) * 512], 19 + 2 * g)
        cast(wsc[20 + 2 * g].rearrange("p (j m) -> p j m", j=4), w_dn_v[:, 4 * g:4 * g + 4, :], 20 + 2 * g)

    emit_casts_upto(2)
    P.dma("sp", cf, cst_f, sem="c_cf", reads=[], writes=["cf"])
    P.dma("pool", cb, cst_b, sem="c_cb", reads=[], writes=["cb"])
    P.dma("pool", toepb.rearrange("p h j -> p (h j)"), toep, sem="c_toep", reads=[], writes=["toepb"])
    P.dma("sp", glwW.rearrange("p c j -> p (c j)"), glw, sem="c_glw", reads=["glw"], writes=["glwW"])
    P.op("dve", lambda e: e.memset(V.rearrange("p a h d -> p (a h d)"), 1.0), writes=[f"V{i}" for i in range(NTILE)])
    P.op("pool", lambda e: e.memset(glowT, 1.0), writes=["glowT"])
    P.op("pool", lambda e: e.memset(QT.rearrange("p h t -> p (h t)"), 0.0), writes=[f"QT{i}" for i in range(8)])
    P.op("pool", lambda e: e.memset(maskT.rearrange("p h t -> p (h t)"), 0.0), writes=["maskT"])
    for _i in range(2):
        P.op("pool", lambda e, _i=_i: e.memset(recip2[_i], 0.0), writes=[f"recip{_i}"])
    P.op("pool", lambda e: e.memset(yAT8.rearrange("p h t -> p (h t)"), 0.0), writes=["yAT8"])
    P.op("pool", lambda e: e.memset(state.rearrange("p h v -> p (h v)"), 0.0), writes=[f"state{i}" for i in range(4)])
    P.op("pool", lambda e: e.memset(stateb.rearrange("p h v -> p (h v)"), 0.0), writes=[f"stateb{i}" for i in range(4)])
    P.op("pool", lambda e: e.memset(kmT.rearrange("p a n -> p (a n)"), 0.0), writes=[f"kmT{i}" for i in range(4)])

    rg = dict(next_load=0, cur=0)
    total_chunks = ntiles * NCHUNK

    def ring_load():
        j = rg["next_load"]
        if j >= total_chunks:
            return
        emit_casts_upto(j + 3)
        rg["next_load"] += 1
        k = j % NCHUNK
        slot = j % NB
        if k in (15, 16):
            P.dma("sp", ring[0:64, slot, :], wsc[k, 0:64, :], sem=f"ring{slot}", reads=[f"wsc{k}"],
                  writes=[f"ring{slot}"])
        else:
            P.dma("sp", ring[:, slot, :], wsc[k], sem=f"ring{slot}", reads=[f"wsc{k}"], writes=[f"ring{slot}"])

    def ring_get(expect_k, ahead=0):
        j = rg["cur"] + ahead
        assert j % NCHUNK == expect_k, (j, expect_k)
        assert j < rg["next_load"], "chunk not yet scheduled"
        slot = j % NB
        return ring[:, slot, :], f"ring{slot}"

    def ring_done():
        rg["cur"] += 1
        ring_load()

    P.dma("sp", xres2[:, 0], xv[0], sem="xld0", reads=[], writes=["xres0"])
    for _ in range(NB):
        ring_load()

    def act(out_, in_, func, R, W, bias=0.0, scale=1.0, accum=None):
        if accum is None:
            P.op("act", lambda e: e.activation(out=out_, in_=in_, func=func, bias=bias, scale=scale), R, W)
        else:
            P.op("act", lambda e: e.activation(out=out_, in_=in_, func=func, bias=bias, scale=scale,
                                               accum_out=accum), R, W)

    def mm(out_, lhsT, rhs, start, stop, R, W):
        P.op("pe", lambda e: e.matmul(out_, lhsT=lhsT, rhs=rhs, start=start, stop=stop), R, W)

    def tr(out_, in_, ident, R, W):
        P.op("pe", lambda e: e.transpose(out=out_, in_=in_, identity=ident), R, W)

    def tt(eng, out_, in0, in1, op, R, W):
        P.op(eng, lambda e: e.tensor_tensor(out=out_, in0=in0, in1=in1, op=op), R, W)

    def ts(eng, out_, in0, s1, s2, op0, op1, R, W):
        P.op(eng, lambda e: e.tensor_scalar(out=out_, in0=in0, scalar1=s1, scalar2=s2, op0=op0, op1=op1), R, W)

    def stt(out_, in0, scalar, in1, op0, op1, R, W, accum=None):
        if accum is None:
            P.op("dve", lambda e: e.scalar_tensor_tensor(out=out_, in0=in0, scalar=scalar, in1=in1, op0=op0, op1=op1),
                 R, W)
        else:
            P.op("dve", lambda e: e.scalar_tensor_tensor(out=out_, in0=in0, scalar=scalar, in1=in1, op0=op0, op1=op1,
                                                         accum_out=accum), R, W)

    tokb2 = [tokf.bitcast(BF16)[:, 0:D], tokf.bitcast(BF16)[:, D:2 * D]]

    def norm_stats(xres, xrn):
        for s in range(2):
            junk, jn = f5n()
            act(junk.bitcast(BF16)[:, 0:D], xres[:, s, :], AF.Square, [xrn], [jn, f"small_ss{s}"],
                accum=small[:, 20 + s:21 + s])
        act(small[:, 22:24], small[:, 20:22], AF.Ln, ["small_ss0", "small_ss1"], ["small_ln"], bias=EPS, scale=1.0 / D)
        act(small[:, 24:26], small[:, 22:24], AF.Exp, ["small_ln"], ["small_rs"], scale=-0.5)
        for s in range(2):
            ts("dve", tokb2[s], xres[:, s, :], small[:, 24 + s:25 + s], None, ALU.mult, ALU.bypass,
               [xrn, "small_rs"], [f"tok{s}"])

    def norm_T(gname):
        gT = C(gname)
        for s in range(2):
            for half in range(2):
                pb, pn = rot()
                pbb = pb.bitcast(BF16)
                for c4 in range(4):
                    c = half * 4 + c4
                    tr(pbb[:, c4 * 128:(c4 + 1) * 128], tokb2[s][:, c * 128:(c + 1) * 128], identb,
                       [f"tok{s}", "cb"], [pn])
                tt("dve", nT[:, half * 4:(half + 1) * 4, s * 128:(s + 1) * 128],
                   pbb[:, 0:512].rearrange("p (c t) -> p c t", c=4),
                   gT[:, half * 4:(half + 1) * 4].unsqueeze(2).to_broadcast([128, 4, 128]), ALU.mult,
                   [pn, "cf"], ["nT"])

    def rms_norm_to_T(gname, t, xres, xrn):
        norm_stats(xres, xrn)
        norm_T(gname)

    def proj_fm(w, wn, col0, ncols, out_ps, pn, rhs_cols=None):
        w3 = w.rearrange("p (c j) -> p c j", c=8)
        for c in range(8):
            mm(out_ps, w3[:, c, col0:col0 + ncols], nT[:, c, :], c == 0, c == 7, [wn, "nT"], [pn])

    def proj_tm(w, wn, s, out_ps, pn):
        w3 = w.rearrange("p (c j) -> p c j", c=8)
        for c in range(8):
            mm(out_ps, nT[:, c, s * 128:(s + 1) * 128], w3[:, c, :], c == 0, c == 7, [wn, "nT"], [pn])

    def stage_pt(k, t):
        if stage <= k:
            P.dma("sp", ov[t], xres, sem=f"ost{t % 2}", reads=[xrn], writes=[f"out{t}"])
            raise _Stop()

    try:
      for t in range(ntiles):
        xres = xres2[:, t % 2]
        xrn = f"xres{t % 2}"
        if t + 1 < ntiles:
            P.dma("sp", xres2[:, (t + 1) % 2], xv[t + 1], sem=f"xld{(t + 1) % 2}", reads=[],
                  writes=[f"xres{(t + 1) % 2}"])
        stage_pt(0, t)
        if t == 0 or stage < 99:
            rms_norm_to_T("g1T", t, xres, xrn)
        stage_pt(1, t)
        dump("nT", nT, t, ["nT"])

        pb, pn = rot()
        for c in range(8):
            mm(pb[0:16, 0:T], glwW[:, c, :], nT[:, c, :], c == 0, c == 7, ["glwW", "nT"], [pn])
        P.op("dve", lambda e, pb=pb: e.tensor_copy(out=glowT[0:16, :], in_=pb[0:16, 0:T]), [pn], ["glowT"])

        stage_pt(1.1, t)
        def qk_finish(p):
            which, hp, qf, qfn, slot = p
            gcol = C("gqc") if which == 0 else C("gkc")
            pb2, pn2 = rot()
            mm(pb2[:, 0:T], blockones, sq3[slot][0], True, True, ["cb", sq3[slot][1]], [pn2])
            act(qf[:, T:2 * T], pb2[:, 0:T], AF.Ln, [pn2], [qfn + "b"], bias=EPS, scale=1.0 / 64)
            if which == 0:
                act(qf[:, T:2 * T], qf[:, T:2 * T], AF.Exp, [qfn + "b"], [qfn + "b"], scale=-0.5,
                    bias=float(np.log(0.125)))
                for par in range(2):
                    r0 = 64 * par
                    stt(QT[r0:r0 + 64, 2 * hp + par, :], qf[r0:r0 + 64, 0:T], gcol[r0:r0 + 64, :],
                        qf[r0:r0 + 64, T:2 * T], ALU.mult, ALU.mult, [qfn, qfn + "b", "cf"], [f"QT{2 * hp + par}"])
            else:
                act(qf[:, T:2 * T], qf[:, T:2 * T], AF.Exp, [qfn + "b"], [qfn + "b"], scale=-0.5)
                stt(KT[:, hp, t * T:(t + 1) * T], qf[:, 0:T], gcol, qf[:, T:2 * T], ALU.mult, ALU.mult,
                    [qfn, qfn + "b", "cf"], [f"KT{hp}_{t}", "ksum"], accum=ksum[:, hp:hp + 1])
                ts("dve", kmT[:, hp, t:t + 1], ksum[:, hp:hp + 1], 1.0 / 256, None, ALU.mult, ALU.bypass,
                   ["ksum"], [f"kmT{hp}"])

        pend = []
        sq3 = [(sqb[:, 0, :], "sqb0"), (sqb[:, 1, :], "sqb1"), (PT[2][:, 0:T], "PT2")]
        for which in range(2):
            w, wn = ring_get(which)
            for hp in range(4):
                pb, pn = rot()
                proj_fm(w, wn, hp * 128, 128, pb[:, 0:T], pn)
                qf, qfn = f5n()
                slot = (which * 4 + hp) % 3
                P.op("dve", lambda e, qf=qf, pb=pb: e.tensor_copy(out=qf[:, 0:T], in_=pb[:, 0:T]), [pn], [qfn])
                tt("pool", sq3[slot][0], qf[:, 0:T], qf[:, 0:T], ALU.mult, [qfn], [sq3[slot][1]])
                if len(pend) == 2:
                    qk_finish(pend.pop(0))
                pend.append((which, hp, qf, qfn, slot))
            ring_done()
        while pend:
            qk_finish(pend.pop(0))
        dump("QT", QT, t, [f"QT{i}" for i in range(8)])
        dump("KT", KT[:, :, t * T:(t + 1) * T], t, [f"KT{i}_{t}" for i in range(4)])

        stage_pt(1.2, t)
        w, wn = ring_get(2)
        for s in range(2):
            pb, pn = rot()
            proj_tm(w, wn, s, pb, pn)
            act(V[:, 2 * t + s, :, 0:64], pb.rearrange("p (h d) -> p h d", h=8), AF.Copy, [pn], [f"V{t}"])
        ring_done()

        stage_pt(1.3, t)
        w, wn = ring_get(3)
        for h in range(4):
            pb, pn = rot()
            proj_fm(w, wn, h * 128, 128, pb[:, 0:T], pn)
            act(gqT[:, h, :], pb[:, 0:T], AF.Copy, [pn], ["gqT"])
        ring_done()
        w, wn = ring_get(4)
        for h in range(4):
            pb, pn = rot()
            proj_fm(w, wn, h * 128, 128, pb[:, 0:T], pn)
            P.op("dve", lambda e, pb=pb, h=h: e.tensor_copy(out=gkT[:, h, :], in_=pb[:, 0:T]), [pn], ["gkT"])
        ring_done()
        for half in range(2):
            w, wn = ring_get(5 + half)
            for s in range(2):
                pb, pn = rot()
                proj_tm(w, wn, s, pb, pn)
                P.op("dve", lambda e, pb=pb, s=s, half=half: e.tensor_copy(
                    out=gv[:, s, half * 512:(half + 1) * 512], in_=pb), [pn], ["gv"])
            ring_done()
        stage_pt(1.4, t)
        for half in range(2):
            w, wn = ring_get(7 + half)
            for s in range(2):
                pb, pn = rot()
                proj_tm(w, wn, s, pb, pn)
                tm, tmn = f5n()
                act(tm, pb, AF.Sigmoid, [pn], [tmn])
                tt("dve", G2[:, s, half * 512:(half + 1) * 512], pb, tm, ALU.mult, [pn, tmn], ["G2"])
            ring_done()
        stage_pt(1.5, t)
        def gla_gen():
            for s in range(2):
                sl = slice(s * 128, (s + 1) * 128)
                pb, pn = rotM()
                mm(pb, glowT[:, sl], C("gup", 32), True, True, ["glowT", "cf"], [pn])
                act(erb, pb, AF.Exp, [pn], ["erb"], scale=-1.0)
                yield
                act(spb, erb, AF.Ln, ["erb"], ["spb"], bias=1.0)
                yield
                pbT, pnT = rotM()
                for h in range(4):
                    mm(pbT[:, h * 128:(h + 1) * 128], spb[:, h * 128:(h + 1) * 128], C("tri_incl"), True, True,
                       ["spb", "cf"], [pnT])
                act(ebT.rearrange("p h c -> p (h c)"), pbT, AF.Exp, [pnT], ["ebT"])
                act(einvT.rearrange("p h c -> p (h c)"), pbT, AF.Exp, [pnT], ["einvT"], scale=-1.0)
                yield
                stt(qdT, gqT[:, :, sl], float(128 ** -0.5), ebT, ALU.mult, ALU.mult, ["gqT", "ebT"], ["qdT"])
                tt("dve", kiT, gkT[:, :, sl], einvT, ALU.mult, ["gkT", "einvT"], ["kiT"])
                yield
                for h in range(4):
                    ts("dve", kdT[:, h, :], kiT[:, h, :], ebT[:, h, 127:128], None, ALU.mult, ALU.bypass,
                       ["kiT", "ebT"], ["gktok"])
                yield
                pbk, pnk = rotM()
                pbkb = pbk.bitcast(BF16)
                for h in range(4):
                    tr(pbkb[:, h * 128:(h + 1) * 128], kdT[:, h, :], identb, ["gktok", "cb"], [pnk])
                P.op("dve", lambda e, pbkb=pbkb: e.tensor_copy(out=kdec, in_=pbkb[:, 0:512]), [pnk], ["kdec"])
                yield
                pba, pna = rotM()
                for h in range(4):
                    mm(pba[:, h * 128:(h + 1) * 128], kiT[:, h, :], qdT[:, h, :], True, True, ["kiT", "qdT"], [pna])
                tt("dve", attm, pba.rearrange("p (h c) -> p h c", h=4),
                   C("causal").unsqueeze(1).to_broadcast([128, 4, 128]), ALU.mult, [pna, "cf"], ["attm"])
                yield
                for hh in range(2):
                    for h in (2 * hh, 2 * hh + 1):
                        reg = ps[6][:, (h % 2) * 256:(h % 2 + 1) * 256]
                        mm(reg, attm[:, h, :], gv[:, s, h * 256:(h + 1) * 256], True, False, ["attm", "gv"], ["ps6"])
                        mm(reg, qdT[:, h, :], stateb[:, h, :], False, True, ["qdT", f"stateb{h}"], ["ps6"])
                    for h in (2 * hh, 2 * hh + 1):
                        mm(ps[7][:, (h % 2) * 256:(h % 2 + 1) * 256], kdec[:, h * 128:(h + 1) * 128],
                           gv[:, s, h * 256:(h + 1) * 256], True, True, ["kdec", "gv"], ["ps7"])
                    yield
                    for h in (2 * hh, 2 * hh + 1):
                        stt(state[:, h, :], state[:, h, :], ebT[:, h, 127:128], ps[7][:, (h % 2) * 256:(h % 2 + 1) * 256],
                            ALU.mult, ALU.add, [f"state{h}", "ebT", "ps7"], [f"state{h}"])
                        P.op("pool", lambda e, h=h: e.tensor_copy(out=stateb[:, h, :], in_=state[:, h, :]),
                             [f"state{h}"], [f"stateb{h}"])
                    sqt, sqtn = f5n()
                    for h in (2 * hh, 2 * hh + 1):
                        reg = ps[6][:, (h % 2) * 256:(h % 2 + 1) * 256]
                        act(sqt[:, (h % 2) * 256:(h % 2 + 1) * 256], reg, AF.Square, ["ps6"], [sqtn, f"small_o{h}"],
                            accum=small[:, 8 + h:9 + h])
                    yield
                    act(small[:, 12 + 2 * hh:14 + 2 * hh], small[:, 8 + 2 * hh:10 + 2 * hh], AF.Ln,
                        [f"small_o{2 * hh}", f"small_o{2 * hh + 1}"], [f"small_ol{hh}"], bias=EPS, scale=1.0 / 256)
                    yield
                    act(small[:, 16 + 2 * hh:18 + 2 * hh], small[:, 12 + 2 * hh:14 + 2 * hh], AF.Exp,
                        [f"small_ol{hh}"], [f"small_or{hh}"], scale=-0.5)
                    yield
                    for h in (2 * hh, 2 * hh + 1):
                        reg = ps[6][:, (h % 2) * 256:(h % 2 + 1) * 256]
                        stt(tokb[:, h * 256:(h + 1) * 256], reg, small[:, 16 + h:17 + h], G2[:, s, h * 256:(h + 1) * 256],
                            ALU.mult, ALU.mult, ["ps6", f"small_or{hh}", "G2"], ["tok0"])
                    yield
                for half in range(2):
                    pb, pn = rotM()
                    pbb = pb.bitcast(BF16)
                    for c4 in range(4):
                        c = half * 4 + c4
                        tr(pbb[:, c4 * 128:(c4 + 1) * 128], tokb[:, c * 128:(c + 1) * 128], identb, ["tok0", "cb"], [pn])
                    tt("dve", nT[:, half * 4:(half + 1) * 4, sl], pbb[:, 0:512].rearrange("p (c t) -> p c t", c=4),
                       C("gonT")[:, half * 4:(half + 1) * 4].unsqueeze(2).to_broadcast([128, 4, 128]), ALU.mult,
                       [pn, "cf"], ["nT"])
                    yield

        gla = gla_gen() if stage > 3 else iter(())
        for which in range(2):
            for half in range(2):
                w, wn = ring_get(9 + 2 * which + half)
                for cc in range(4):
                    c = half * 4 + cc
                    pb, pn = rot()
                    proj_fm(w, wn, cc * 128, 128, pb[:, 0:T], pn)
                    act(sigT[:, which, c, :], pb[:, 0:T], AF.Sigmoid, [pn], ["sigT"])
                ring_done()
        dump("sigT", sigT, t, ["sigT"])
        dump("G2", G2, t, ["G2"])
        dump("gqT", gqT, t, ["gqT"])

        stage_pt(2, t)
        ob = t
        use_mask = ob >= 4
        if use_mask:
            vbw = C("vbvec")[:, 15 - ob:31 - ob]
            for s in range(2):
                pbs = [rot(), rot()]
                for h in range(8):
                    b0 = (h % 2) * 64
                    pb, pn = pbs[h % 2]
                    mm(pb[:, (h // 2) * 16:(h // 2 + 1) * 16], QT[:, h, s * 128:(s + 1) * 128],
                       kmT[:, h // 2, :], True, True, [f"QT{h}", f"kmT{h // 2}"], [pn])
                gp4 = gp.rearrange("p (a b) n -> p a b n", b=2)
                for par in range(2):
                    pb, pn = pbs[par]
                    tt("dve", gp4[:, :, par, :], pb[:, 0:64].rearrange("p (h n) -> p h n", h=4),
                       vbw.unsqueeze(1).to_broadcast([128, 4, 16]), ALU.add, [pn, "cf"], ["gp"])
                for h in range(8):
                    P.op("dve", lambda e, h=h: e.max(out=top8[:, h, :], in_=gp[:, h, :]), ["gp"], ["top8"])
                tt("dve", m01, gp, top8[:, :, 3:4].to_broadcast([128, 8, 16]), ALU.is_ge, ["gp", "top8"], ["m01"])
                ts("dve", mb, m01, 30000.0, -30000.0, ALU.mult, ALU.add, ["m01"], ["mb"])
                pb2, pn2 = rot()
                pbb = pb2.bitcast(BF16)
                for h in range(8):
                    tr(pbb[0:16, h * 128:(h + 1) * 128], mb[:, h, :], identb, ["mb", "cb"], [pn2])
                P.op("dve", lambda e, pbb=pbb, s=s: e.tensor_copy(
                    out=maskT[0:16, :, s * 128:(s + 1) * 128], in_=pbb[0:16, :].rearrange("p (h q) -> p h q", h=8)),
                    [pn2], ["maskT"])
            dump("maskT", maskT, t, ["maskT"])

        items = [(h, n) for h in range(8) for n in range(ob + 1)]

        def stA(h, n):
            hp = h // 2
            sbk, sn = rotS()
            for kt in range(2):
                ktile = 2 * n + kt
                reg = sbk[:, kt * T:(kt + 1) * T]
                extra = (n < ob and use_mask) or (n >= ob - 1)
                mm(reg, KT[:, hp, ktile * 128:(ktile + 1) * 128], QT[:, h, :], True,
                   not extra, [f"KT{hp}_{n}", f"QT{h}"], [sn])
                if n < ob and use_mask:
                    last = not (n >= ob - 1)
                    mm(reg, blockind[:, n * 128:(n + 1) * 128], maskT[:, h, :], False, last, ["cb", "maskT"], [sn])
                if n == ob:
                    j0 = 128 - kt * 128
                    mm(reg, identb, toepb[:, h, j0:j0 + T], False, True, ["cb", "toepb"], [sn])
                elif n == ob - 1:
                    j0 = 128 + 256 - kt * 128
                    mm(reg, identb, toepb[:, h, j0:j0 + T], False, True, ["cb", "toepb"], [sn])
            return sbk, sn

        def stB(h, n, sbk, sn):
            pt = PT[st["pt"] % 3]
            ptn = f"PT{st['pt'] % 3}"
            st["pt"] += 1
            if n <= ob - 2:
                act(pt, sbk, AF.Exp, [sn, "cf"], [ptn], bias=C("t5far")[:, h:h + 1])
            else:
                act(pt, sbk, AF.Exp, [sn], [ptn])
            return pt, ptn

        def stC(h, n, pt, ptn):
            if n == 0:
                norm_flush(h % 2)
            oi = 4 + (h % 2)
            Oacc = ps[oi]
            on = f"ps{oi}"
            for kt in range(2):
                ktile = 2 * n + kt
                mm(Oacc[0:65, 0:T], V[:, ktile, h, :], pt[:, kt * T:(kt + 1) * T],
                   n == 0 and kt == 0, n == ob and kt == 1, [f"V{n}", ptn], [on])
            if n == ob:
                norm_q[h % 2] = norm_gen(h)

        def norm_gen(h):
            oi = 4 + (h % 2)
            Oacc = ps[oi]
            on = f"ps{oi}"
            recip = recip2[h % 2]
            rcn = f"recip{h % 2}"
            act(recip[64:65, :], Oacc[64:65, 0:T], AF.Ln, [on], [rcn])
            yield
            act(recip[64:65, :], recip[64:65, :], AF.Exp, [rcn], [rcn], scale=-1.0)
            yield
            mm(Oacc[0:64, T:2 * T], C("ones"), recip, True, True, ["cf", rcn], [on])
            yield
            yield
            bc, bcn = f5n()
            P.op("dve", lambda e: e.tensor_copy(out=bc[0:64, 0:T], in_=Oacc[0:64, T:2 * T]), [on], [bcn])
            tt("dve", yAT8[0:64, h, :], Oacc[0:64, 0:T], bc[0:64, 0:T], ALU.mult, [on, bcn], ["yAT8"])

        norm_q = {0: None, 1: None}

        def norm_flush(par):
            g = norm_q[par]
            if g is not None:
                for _ in g:
                    pass
                norm_q[par] = None

        def norm_step():
            for par in (0, 1):
                g = norm_q[par]
                if g is not None:
                    try:
                        next(g)
                    except StopIteration:
                        norm_q[par] = None

        LOOK = 2
        gla_every = max(1, len(items) // 44)
        Ares = {}
        for i in range(len(items) + LOOK):
            if i < len(items):
                Ares[i] = stA(*items[i])
            j = i - LOOK
            if j >= 0:
                pt, ptn = stB(*items[j], *Ares.pop(j))
                norm_step()
                stC(*items[j], pt, ptn)
            if i % gla_every == 0:
                next(gla, None)
        norm_flush(0)
        norm_flush(1)
        for _ in gla:
            pass
        dump("yAT8", yAT8, t, ["yAT8"])
        dump("yGT", nT, t, ["nT"])
        dump("state", state, t, [f"state{h}" for h in range(4)])

        stage_pt(4, t)
        mixedT = gqT.rearrange("p a t -> p (a t)").bitcast(BF16).rearrange("p (c t) -> p c t", c=8)
        for hf in range(2):
            wG, wGn = ring_get(13 + hf)
            wG3 = wG.rearrange("p (c j) -> p c j", c=8)
            for cc in range(4):
                c = hf * 4 + cc
                pg, pgn = rot()
                for j in range(8):
                    mm(pg[:, 0:T], wG3[:, j, cc * 128:(cc + 1) * 128], nT[:, j, :], j == 0, j == 7,
                       [wGn, "nT"], [pgn])
                tt("dve", mixedT[:, c, :], pg[:, 0:T], sigT[:, 1, c, :], ALU.mult, [pgn, "sigT"], ["gqT"])
            ring_done()
        for hf in range(2):
            wA, wAn = ring_get(15 + hf)
            wA3 = wA.rearrange("p (h j) -> p h j", h=8)
            for cc in range(4):
                c = hf * 4 + cc
                pa, pan = rot()
                for h in range(8):
                    mm(pa[:, 0:T], wA3[:, h, cc * 128:(cc + 1) * 128], yAT8[:, h, :], h == 0, h == 7,
                       [wAn, "yAT8"], [pan])
                tm, tmn = f5n()
                tt("dve", tm[:, 0:T], pa[:, 0:T], sigT[:, 0, c, :], ALU.mult, [pan, "sigT"], [tmn])
                tt("pool", mixedT[:, c, :], tm[:, 0:T], mixedT[:, c, :], ALU.add, [tmn, "gqT"], ["gqT"])
            ring_done()
        dump("mixedT", mixedT, t, ["gqT"])
        for hf in range(2):
            w, wn = ring_get(17 + hf)
            w3 = w.rearrange("p (c j) -> p c j", c=8)
            for s in range(2):
                pb, pn = rot()
                for c in range(8):
                    mm(pb, mixedT[:, c, s * 128:(s + 1) * 128], w3[:, c, :], c == 0, c == 7, [wn, "gqT"], [pn])
                tt("dve", xres[:, s, hf * 512:(hf + 1) * 512], pb, xres[:, s, hf * 512:(hf + 1) * 512], ALU.add,
                   [pn, xrn], [xrn])
            ring_done()
        dump("h1", xres, t, [xrn])

        stage_pt(5, t)
        rms_norm_to_T("g2T", t, xres, xrn)
        stage_pt(5.5, t)
        for g in range(8):
            if g == 1:
                stage_pt(5.7, t)
            if g == 6 and t + 1 < ntiles and stage >= 99:
                norm_stats(xres2[:, (t + 1) % 2], f"xres{(t + 1) % 2}")
            wu, wun = ring_get(19 + 2 * g)
            wu3 = wu.rearrange("p (c j) -> p c j", c=8)
            ats = []
            for j in range(4):
                pb, pn = rot()
                for c in range(8):
                    mm(pb[:, 0:T], wu3[:, c, j * 128:(j + 1) * 128], nT[:, c, :], c == 0, c == 7, [wun, "nT"], [pn])
                a = PT[(st["at"] % 4) // 2][:, (st["at"] % 2) * T:(st["at"] % 2 + 1) * T]
                an = f"PT{(st['at'] % 4) // 2}"
                st["at"] += 1
                rl, rln = f5n()
                act(rl[:, 0:T], pb[:, 0:T], AF.Relu, [pn], [rln])
                tt("pool", a, rl[:, 0:T], rl[:, 0:T], ALU.mult, [rln], [an])
                ats.append((a, an))
            ring_done()
            stage_pt(5.6, t)
            if g == 7 and t + 1 < ntiles and stage >= 99:
                norm_T("g1T")
            wd, wdn = ring_get(20 + 2 * g)
            wd3 = wd.rearrange("p (j m) -> p j m", j=4)
            for j in range(4):
                a, an = ats[j]
                first = (g == 0 and j == 0)
                last = (g == 7 and j == 3)
                for s in range(2):
                    for hf in range(2):
                        bi = 4 + s * 2 + hf
                        mm(ps[bi], a[:, s * 128:(s + 1) * 128], wd3[:, j, hf * 512:(hf + 1) * 512], first, last,
                           [wdn, an], [f"ps{bi}"])
            ring_done()
        for s in range(2):
            for hf in range(2):
                bi = 4 + s * 2 + hf
                tt("dve", xres[:, s, hf * 512:(hf + 1) * 512], ps[bi], xres[:, s, hf * 512:(hf + 1) * 512], ALU.add,
                   [f"ps{bi}", xrn], [xrn])
        P.dma("sp", ov[t], xres, sem=f"ost{t % 2}", reads=[xrn], writes=[f"out{t}"])
    except _Stop:
        pass
    P.finish("sp", [f"out{t}" for t in range(ntiles)])
    P.emit()
    return nc, dbg_outs


def _t5_bucket_np(d):
    d = np.maximum(d, 0)
    large = 16 + (np.log(np.maximum(d, 1).astype(np.float32) / 16) / np.log(128 / 16) * 16).astype(np.int32)
    large = np.minimum(large, 31)
    return np.where(d < 16, d, large)


def host_consts(norm1_g, norm2_g, moba_q_norm_g, moba_k_norm_g, t5_bias, gla_gate_up, gla_gate_bias,
                gla_out_norm_g):
    cf = np.zeros((128, NCF), np.float32)

    def put(name, arr):
        o, w = CF[name]
        cf[:, o:o + w] = arr

    put("g1T", np.asarray(norm1_g, np.float32).reshape(8, 128).T)
    put("g2T", np.asarray(norm2_g, np.float32).reshape(8, 128).T)
    put("gonT", np.tile(np.asarray(gla_out_norm_g, np.float32).reshape(2, 128).T, (1, 4)))
    put("gqc", np.tile(np.asarray(moba_q_norm_g, np.float32).reshape(64), 2).reshape(128, 1))
    put("gkc", np.tile(np.asarray(moba_k_norm_g, np.float32).reshape(64), 2).reshape(128, 1))
    put("t5far", np.broadcast_to(np.asarray(t5_bias, np.float32)[31], (128, 8)))
    vb = np.array([0.0] * 15 + [1e30] + [-1e30] * 15, np.float32)
    put("vbvec", np.broadcast_to(vb, (128, 31)))
    e = np.arange(128)[:, None]
    c = np.arange(128)[None, :]
    put("tri_incl", np.where(e <= c, -1.0 / 16, 0.0).astype(np.float32))
    put("tri_strict", np.where(e > c, -1.0 / 16, 0.0).astype(np.float32))
    put("causal", np.where(c >= e, 1.0, 0.0).astype(np.float32))
    gup = np.zeros((128, 512), np.float32)
    gup[0:16] = np.asarray(gla_gate_up, np.float32).reshape(16, 512)
    gup[16] = np.asarray(gla_gate_bias, np.float32).reshape(512)
    put("gup", gup)
    put("ones", np.ones((128, 64), np.float32))
    put("identf", np.eye(128, dtype=np.float32))

    cbm = np.zeros((128, NCB), np.float32)
    o, w = CB["identb"]
    cbm[:, o:o + w] = np.eye(128, dtype=np.float32)
    o, w = CB["blockones"]
    cbm[:, o:o + w] = (e // 64 == c // 64).astype(np.float32)
    o, w = CB["blockind"]
    bi = np.zeros((128, 2048), np.float32)
    for n in range(16):
        bi[n, n * 128:(n + 1) * 128] = 1.0
    cbm[:, o:o + w] = bi

    t5ext = np.concatenate([np.asarray(t5_bias, np.float32), np.full((1, 8), -30000.0, np.float32)], axis=0)
    k = np.arange(128)[:, None]
    jp = np.arange(640)[None, :]
    d = jp - 128 - k
    idx = np.where(d >= 0, _t5_bucket_np(d), 32)
    toep = np.ascontiguousarray(t5ext[idx].transpose(0, 2, 1)).reshape(128, 8 * 640)
    return cf, cbm, toep


_NC_CACHE = {}


def kernel(x, norm1_g, w_in, moba_q_norm_g, moba_k_norm_g, t5_bias, gla_gate_up, gla_gate_bias,
           gla_out_norm_g, w_branch_moba, w_branch_gla, w_out, norm2_g, w_up, w_down):
    x = np.asarray(x, np.float32)
    B = x.shape[0]
    cf, cbm, toep = host_consts(np.asarray(norm1_g)[0], np.asarray(norm2_g)[0], np.asarray(moba_q_norm_g)[0],
                                np.asarray(moba_k_norm_g)[0], np.asarray(t5_bias), np.asarray(gla_gate_up)[0],
                                np.asarray(gla_gate_bias)[0], np.asarray(gla_out_norm_g)[0])
    shared = {
        "w_in": np.ascontiguousarray(np.asarray(w_in, np.float32)[0]),
        "w_bm": np.ascontiguousarray(np.asarray(w_branch_moba, np.float32)[0]),
        "w_bg": np.ascontiguousarray(np.asarray(w_branch_gla, np.float32)[0]),
        "w_out": np.ascontiguousarray(np.asarray(w_out, np.float32)[0]),
        "w_up": np.ascontiguousarray(np.asarray(w_up, np.float32)[0]),
        "w_down": np.ascontiguousarray(np.asarray(w_down, np.float32)[0]),
        "cst_f": cf, "cst_b": cbm, "toep": toep,
    }
    if "nc" not in _NC_CACHE:
        _NC_CACHE["nc"] = build_nc()[0]
    nc = _NC_CACHE["nc"]
    in_maps = [dict(shared, x=np.ascontiguousarray(x[i])) for i in range(B)]
    res = run_bass_kernel_spmd(nc, in_maps, core_ids=list(range(B)))
    return np.stack([np.asarray(r["out"], np.float32) for r in res.results], axis=0)
```
